# Optimizing a Trainium2 kernel written in Bass

```python
import math
import jax
import jax.numpy as jnp
from jax import lax
import numpy as np

D_MODEL = 1024
BATCH = 16
SEQ = 256
DEPTH = 2
DEC_BATCH = 8
DEC_SEQ = 2048
PAST_LEN = 512

GRID_W = 64
ROPE_THETA = 10000.0
Q_BLOCK = 128
EPS = 1e-6
MLA_HEADS = 8
MLA_NOPE = 64
MLA_ROPE = 32
MLA_V = 64
MLA_Q_LORA = 384
MLA_KV_LORA = 256
MLA_WIDTH = MLA_HEADS * MLA_V
MLA_SCALE = (MLA_NOPE + MLA_ROPE) ** -0.5
SWA_HEADS = 8
SWA_KV_HEADS = 2
SWA_DH = 64
SWA_WIDTH = SWA_HEADS * SWA_DH
WINDOW = 128
SWA_SCALE = SWA_DH ** -0.5
GDN_HEADS = 4
GDN_DK = 128
GDN_DV = 128
GDN_WIDTH = GDN_HEADS * GDN_DV
CONV_K = 3
GDN_CHUNK = 64
ATT_HEADS = 4
ATT_KV_HEADS = 2
ATT_DH = 128
ATT_WIDTH = ATT_HEADS * ATT_DH
ATT_SCALE = ATT_DH ** -0.5
L0_SPLIT = (MLA_Q_LORA, MLA_KV_LORA, MLA_ROPE, MLA_WIDTH, SWA_WIDTH, SWA_KV_HEADS * SWA_DH, SWA_KV_HEADS * SWA_DH, SWA_WIDTH)
L1_SPLIT = (GDN_HEADS * (2 * GDN_DK + GDN_DV), 2 * GDN_HEADS, 2 * GDN_HEADS, GDN_WIDTH, ATT_WIDTH, ATT_KV_HEADS * ATT_DH, ATT_KV_HEADS * ATT_DH, ATT_WIDTH)
L0_COLS = sum(L0_SPLIT)
L1_COLS = sum(L1_SPLIT)
NEG = -1e30

kernel_name = 'hybrid_mla_swa_gdn_axial_step'


def _rms(x, g):
    xf = x.astype(jnp.float32)
    y = xf * lax.rsqrt(jnp.mean(xf * xf, axis=-1, keepdims=True) + EPS)
    return (y * g.astype(jnp.float32)).astype(x.dtype)


def _l2(x):
    xf = x.astype(jnp.float32)
    return (xf * lax.rsqrt(jnp.sum(xf * xf, axis=-1, keepdims=True) + EPS)).astype(x.dtype)


def _split(u, sizes):
    cuts, acc = [], 0
    for s in sizes[:-1]:
        acc += s
        cuts.append(acc)
    return jnp.split(u, cuts, axis=-1)


def _adaln(x, gain, mod):
    shift, scale, gate = jnp.split(mod, 3, axis=-1)
    return _rms(x, gain) * (1 + scale) + shift, gate


def _group(q, n_kv):
    B, T, H, d = q.shape
    return q.reshape(B, T, n_kv, H // n_kv, d)


def _rope_tables(n_tok, rot_dim):
    rows = n_tok // GRID_W
    quarter = rot_dim // 4
    inv = ROPE_THETA ** (-jnp.arange(quarter, dtype=jnp.float32) / quarter)
    t = jnp.arange(rows * GRID_W)
    pos = jnp.stack([t // GRID_W, t % GRID_W], axis=-1).astype(jnp.float32)
    ang = pos[:, :, None] * inv
    return jnp.cos(ang), jnp.sin(ang)


def _apply_rope(x, cos, sin):
    B, T, H, R = x.shape
    xs = x.astype(jnp.float32).reshape(B, T, H, 2, 2, R // 4)
    x1, x2 = xs[..., 0, :], xs[..., 1, :]
    c, s = cos[None, :, None], sin[None, :, None]
    out = jnp.stack([x1 * c - x2 * s, x2 * c + x1 * s], axis=-2)
    return out.reshape(B, T, H, R).astype(x.dtype)


def _attend(q, k, v, mask, sink, scale):
    s = jnp.einsum('bqhgd,bkhd->bhgqk', q, k).astype(jnp.float32) * scale
    if mask is not None:
        s = jnp.where(mask, s, NEG)
    if sink is not None:
        sk = jnp.broadcast_to(sink.astype(jnp.float32)[None, :, :, None, None], s.shape[:-1] + (1,))
        p = jax.nn.softmax(jnp.concatenate([s, sk], axis=-1), axis=-1)[..., :-1]
    else:
        p = jax.nn.softmax(s, axis=-1)
    return jnp.einsum('bhgqk,bkhd->bqhgd', p.astype(v.dtype), v)


def _sweep_queries(q, fn):
    B, Q = q.shape[0], q.shape[1]
    nb = Q // Q_BLOCK
    qb = jnp.moveaxis(q.reshape((B, nb, Q_BLOCK) + q.shape[2:]), 1, 0)
    out = lax.map(lambda a: fn(a[0], a[1]), (qb, jnp.arange(nb)))
    return jnp.moveaxis(out, 0, 1).reshape((B, Q) + out.shape[3:])


def _window_ctx_attend(q, k, v, k_ctx, v_ctx, sink):
    S, L = k.shape[1], k_ctx.shape[1]
    span = Q_BLOCK + 2 * WINDOW
    pad = ((0, 0), (WINDOW, WINDOW), (0, 0), (0, 0))
    kp, vp = jnp.pad(k, pad), jnp.pad(v, pad)
    a = jnp.arange(Q_BLOCK)[:, None]
    b = jnp.arange(span)[None, :]
    band = (b - a >= 0) & (b - a <= 2 * WINDOW)
    ctx_mask = jnp.ones((Q_BLOCK, L), dtype=bool)

    def blk(qb, n):
        start = n * Q_BLOCK
        ks = lax.dynamic_slice_in_dim(kp, start, span, axis=1)
        vs = lax.dynamic_slice_in_dim(vp, start, span, axis=1)
        j = start - WINDOW + b
        mask = jnp.concatenate([band & (j >= 0) & (j < S), ctx_mask], axis=1)
        return _attend(qb, jnp.concatenate([ks, k_ctx], axis=1), jnp.concatenate([vs, v_ctx], axis=1), mask, sink, SWA_SCALE)

    return _sweep_queries(q, blk)


def _short_conv(x, w):
    y = lax.conv_general_dilated(x, w[:, None, :].astype(x.dtype), window_strides=(1,),
                                 padding=((CONV_K // 2, CONV_K // 2),),
                                 dimension_numbers=('NWC', 'WIO', 'NWC'), feature_group_count=x.shape[-1])
    return jax.nn.silu(y)


def _gdn_chunked(q, k, v, g, beta, s0):
    out_dtype = v.dtype
    f32 = jnp.float32
    B, T, H, dk = k.shape
    dv = v.shape[-1]
    n = T // GDN_CHUNK

    def chunks(t):
        t = t.astype(f32).reshape((B, n, GDN_CHUNK, H) + t.shape[3:])
        return jnp.moveaxis(t, 3, 1)

    q = chunks(q) * dk ** -0.5
    k, v, beta, g = chunks(k), chunks(v), chunks(beta), chunks(g)
    gc = jnp.cumsum(g, axis=-1)
    causal = jnp.tril(jnp.ones((GDN_CHUNK, GDN_CHUNK), dtype=bool))
    strict = jnp.tril(jnp.ones((GDN_CHUNK, GDN_CHUNK), dtype=bool), -1)
    decay = jnp.exp(jnp.where(causal, gc[..., :, None] - gc[..., None, :], -jnp.inf))
    kb = k * beta[..., None]
    lmat = jnp.where(strict, jnp.einsum('bhncd,bhnsd->bhncs', kb, k) * decay, 0.0)
    amat = lmat + jnp.eye(GDN_CHUNK, dtype=f32)
    rhs = jnp.concatenate([v * beta[..., None], kb * jnp.exp(gc)[..., None]], axis=-1)
    sol = lax.linalg.triangular_solve(amat, rhs, left_side=True, lower=True)
    u, w = sol[..., :dv], sol[..., dv:]
    intra = jnp.einsum('bhncd,bhnsd->bhncs', q, k) * decay

    def step(S, xs):
        q_i, k_i, u_i, w_i, g_i, a_i = xs
        v_new = u_i - jnp.einsum('bhcd,bhde->bhce', w_i, S)
        o = jnp.einsum('bhcd,bhde->bhce', q_i * jnp.exp(g_i)[..., None], S) + jnp.einsum('bhcs,bhse->bhce', a_i, v_new)
        g_last = g_i[..., -1:]
        S = S * jnp.exp(g_last)[..., None] + jnp.einsum('bhcd,bhce->bhde', k_i * jnp.exp(g_last - g_i)[..., None], v_new)
        return S, o

    xs = tuple(jnp.moveaxis(t, 2, 0) for t in (q, k, u, w, gc, intra))
    S, o = lax.scan(step, s0.astype(f32), xs)
    o = jnp.moveaxis(jnp.moveaxis(o, 0, 2), 1, 3).reshape(B, T, H, dv)
    return o.astype(out_dtype), S.astype(out_dtype)


def _bidir_gdn(q, k, v, g, beta, s0):
    o_f, s_f = _gdn_chunked(q, k, v, g[:, :, 0], beta[:, :, 0], s0[:, 0])
    rev = lambda t: jnp.flip(t, axis=1)
    o_b, s_b = _gdn_chunked(rev(q), rev(k), rev(v), rev(g[:, :, 1]), rev(beta[:, :, 1]), s0[:, 1])
    return o_f + rev(o_b), jnp.stack([s_f, s_b], axis=1)


def _mla_expand(ckv, krope, w_ukv):
    B, T, _ = ckv.shape
    kv = (ckv @ w_ukv).reshape(B, T, MLA_HEADS, MLA_NOPE + MLA_V)
    k = jnp.concatenate([kv[..., :MLA_NOPE], jnp.broadcast_to(krope[:, :, None, :], (B, T, MLA_HEADS, MLA_ROPE))], axis=-1)
    return k, kv[..., MLA_NOPE:]


def _layer0(x, mod, ctx, ln, w_in, mla_q_norm, w_uq, mla_kv_norm, w_ukv, swa_sink, w_out):
    B, T, _ = x.shape
    h, gate = _adaln(x, ln, mod)
    cq, ckv, krope, z_a, q_b, k_b, v_b, z_b = _split(h @ w_in, L0_SPLIT)
    q_a = (_rms(cq, mla_q_norm) @ w_uq).reshape(B, T, MLA_HEADS, MLA_NOPE + MLA_ROPE)
    ckv = _rms(ckv, mla_kv_norm)
    q_b = q_b.reshape(B, T, SWA_HEADS, SWA_DH)
    k_b = k_b.reshape(B, T, SWA_KV_HEADS, SWA_DH)
    v_b = v_b.reshape(B, T, SWA_KV_HEADS, SWA_DH)
    sink = swa_sink.reshape(SWA_KV_HEADS, SWA_HEADS // SWA_KV_HEADS)
    if ctx is None:
        k_a, v_a = _mla_expand(ckv, krope, w_ukv)
        o_b = _sweep_queries(_group(q_b, SWA_KV_HEADS), lambda qb, n: _attend(qb, k_b, v_b, None, sink, SWA_SCALE))
        new_ctx = (ckv, krope, k_b, v_b)
    else:
        ckv_c, krope_c, k_c, v_c = ctx
        cos, sin = _rope_tables(T, MLA_ROPE)
        q_a = jnp.concatenate([q_a[..., :MLA_NOPE], _apply_rope(q_a[..., MLA_NOPE:], cos, sin)], axis=-1)
        krope_r = _apply_rope(krope[:, :, None, :], cos, sin)[:, :, 0]
        k_a, v_a = _mla_expand(jnp.concatenate([ckv, ckv_c], axis=1), jnp.concatenate([krope_r, krope_c], axis=1), w_ukv)
        cos, sin = _rope_tables(T, SWA_DH)
        o_b = _window_ctx_attend(_group(_apply_rope(q_b, cos, sin), SWA_KV_HEADS), _apply_rope(k_b, cos, sin), v_b, k_c, v_c, sink)
        new_ctx = ()
    o_a = _sweep_queries(q_a[:, :, :, None, :], lambda qb, n: _attend(qb, k_a, v_a, None, None, MLA_SCALE))
    y = jnp.concatenate([o_a.reshape(B, T, MLA_WIDTH) * jax.nn.silu(z_a),
                         o_b.reshape(B, T, SWA_WIDTH) * jax.nn.silu(z_b)], axis=-1) @ w_out
    return x + gate * y, new_ctx


def _layer1(x, mod, ctx, ln, w_in, gdn_conv, gdn_a_log, gdn_dt_bias, gdn_norm, att_q_norm, att_k_norm, w_out):
    B, T, _ = x.shape
    f32 = jnp.float32
    h, gate = _adaln(x, ln, mod)
    qkv, a, b, z_c, q_d, k_d, v_d, z_d = _split(h @ w_in, L1_SPLIT)
    q_c, k_c, v_c = _split(_short_conv(qkv, gdn_conv), (GDN_HEADS * GDN_DK, GDN_HEADS * GDN_DK, GDN_HEADS * GDN_DV))
    q_c = _l2(q_c.reshape(B, T, GDN_HEADS, GDN_DK))
    k_c = _l2(k_c.reshape(B, T, GDN_HEADS, GDN_DK))
    v_c = v_c.reshape(B, T, GDN_HEADS, GDN_DV)
    a = a.astype(f32).reshape(B, T, 2, GDN_HEADS)
    g = -jnp.exp(gdn_a_log.astype(f32)) * jax.nn.softplus(a + gdn_dt_bias.astype(f32))
    beta = jax.nn.sigmoid(b.astype(f32).reshape(B, T, 2, GDN_HEADS))
    s0 = jnp.zeros((B, 2, GDN_HEADS, GDN_DK, GDN_DV), x.dtype) if ctx is None else ctx[0]
    o_c, s_fin = _bidir_gdn(q_c, k_c, v_c, g, beta, s0)
    o_c = _rms(o_c, gdn_norm) * jax.nn.silu(z_c.reshape(B, T, GDN_HEADS, GDN_DV))
    q_d = _rms(q_d.reshape(B, T, ATT_HEADS, ATT_DH), att_q_norm)
    k_d = _rms(k_d.reshape(B, T, ATT_KV_HEADS, ATT_DH), att_k_norm)
    v_d = v_d.reshape(B, T, ATT_KV_HEADS, ATT_DH)
    if ctx is None:
        q_att, k_all, v_all = q_d, k_d, v_d
        new_ctx = (s_fin, k_d, v_d)
    else:
        cos, sin = _rope_tables(T, ATT_DH)
        q_att = _apply_rope(q_d, cos, sin)
        k_all = jnp.concatenate([_apply_rope(k_d, cos, sin), ctx[1]], axis=1)
        v_all = jnp.concatenate([v_d, ctx[2]], axis=1)
        new_ctx = ()
    o_d = _sweep_queries(_group(q_att, ATT_KV_HEADS), lambda qb, n: _attend(qb, k_all, v_all, None, None, ATT_SCALE))
    y = jnp.concatenate([o_c.reshape(B, T, GDN_WIDTH),
                         o_d.reshape(B, T, ATT_WIDTH) * jax.nn.silu(z_d)], axis=-1) @ w_out
    return x + gate * y, new_ctx


def setup_inputs(seed: int = 0) -> dict:
    key = jax.random.key(seed)
    ks = iter(jax.random.split(key, 48))
    f32 = jnp.float32

    def nrm(shape, scale=1.0):
        return jax.random.normal(next(ks), shape, f32) * scale

    D = D_MODEL
    inp = {}
    inp['x_prompt'] = nrm((BATCH, SEQ, D))
    inp['x_sample'] = nrm((DEC_BATCH, DEC_SEQ, D))
    inp['cache_l0_mla_ckv'] = nrm((DEC_BATCH, PAST_LEN, MLA_KV_LORA))
    inp['cache_l0_mla_krope'] = nrm((DEC_BATCH, PAST_LEN, MLA_ROPE))
    inp['cache_l0_swa_k'] = nrm((DEC_BATCH, PAST_LEN, SWA_KV_HEADS, SWA_DH))
    inp['cache_l0_swa_v'] = nrm((DEC_BATCH, PAST_LEN, SWA_KV_HEADS, SWA_DH))
    inp['state_l1_gdn'] = nrm((DEC_BATCH, 2, GDN_HEADS, GDN_DK, GDN_DV), 0.1)
    inp['cache_l1_attn_k'] = nrm((DEC_BATCH, PAST_LEN, ATT_KV_HEADS, ATT_DH))
    inp['cache_l1_attn_v'] = nrm((DEC_BATCH, PAST_LEN, ATT_KV_HEADS, ATT_DH))
    inp['c'] = nrm((DEC_BATCH, D))
    inp['c_ctx'] = nrm((D,))
    inp['w_mod0'] = nrm((D, 3 * D), 0.3 * D ** -0.5)
    inp['b_mod0'] = nrm((3 * D,), 0.02)
    inp['ln0'] = 1.0 + nrm((D,), 0.1)
    inp['w_in0'] = nrm((D, L0_COLS), D ** -0.5)
    inp['mla_q_norm'] = 1.0 + nrm((MLA_Q_LORA,), 0.1)
    inp['w_uq'] = nrm((MLA_Q_LORA, MLA_HEADS * (MLA_NOPE + MLA_ROPE)), MLA_Q_LORA ** -0.5)
    inp['mla_kv_norm'] = 1.0 + nrm((MLA_KV_LORA,), 0.1)
    inp['w_ukv'] = nrm((MLA_KV_LORA, MLA_HEADS * (MLA_NOPE + MLA_V)), MLA_KV_LORA ** -0.5)
    inp['swa_sink'] = nrm((SWA_HEADS,))
    inp['w_out0'] = nrm((MLA_WIDTH + SWA_WIDTH, D), (MLA_WIDTH + SWA_WIDTH) ** -0.5)
    inp['w_mod1'] = nrm((D, 3 * D), 0.3 * D ** -0.5)
    inp['b_mod1'] = nrm((3 * D,), 0.02)
    inp['ln1'] = 1.0 + nrm((D,), 0.1)
    inp['w_in1'] = nrm((D, L1_COLS), D ** -0.5)
    inp['gdn_conv'] = nrm((CONV_K, GDN_HEADS * (2 * GDN_DK + GDN_DV)), CONV_K ** -0.5)
    inp['gdn_a_log'] = jnp.log(jax.random.uniform(next(ks), (2, GDN_HEADS), f32, 1.0, 16.0))
    dt = jnp.exp(jax.random.uniform(next(ks), (2, GDN_HEADS), f32, math.log(1e-3), math.log(1e-1)))
    inp['gdn_dt_bias'] = dt + jnp.log(-jnp.expm1(-dt))
    inp['gdn_norm'] = 1.0 + nrm((GDN_DV,), 0.1)
    inp['att_q_norm'] = 1.0 + nrm((ATT_DH,), 0.1)
    inp['att_k_norm'] = 1.0 + nrm((ATT_DH,), 0.1)
    inp['w_out1'] = nrm((GDN_WIDTH + ATT_WIDTH, D), (GDN_WIDTH + ATT_WIDTH) ** -0.5)
    inp['ln_f'] = 1.0 + nrm((D,), 0.1)
    return inp


def reference(x_prompt, x_sample, cache_l0_mla_ckv, cache_l0_mla_krope, cache_l0_swa_k, cache_l0_swa_v,
              state_l1_gdn, cache_l1_attn_k, cache_l1_attn_v, c, c_ctx,
              w_mod0, b_mod0, ln0, w_in0, mla_q_norm, w_uq, mla_kv_norm, w_ukv, swa_sink, w_out0,
              w_mod1, b_mod1, ln1, w_in1, gdn_conv, gdn_a_log, gdn_dt_bias, gdn_norm, att_q_norm, att_k_norm, w_out1,
              ln_f):
    layer_fns = (_layer0, _layer1)
    layer_params = ((w_mod0, b_mod0, (ln0, w_in0, mla_q_norm, w_uq, mla_kv_norm, w_ukv, swa_sink, w_out0)),
                    (w_mod1, b_mod1, (ln1, w_in1, gdn_conv, gdn_a_log, gdn_dt_bias, gdn_norm, att_q_norm, att_k_norm, w_out1)))
    caches = ((cache_l0_mla_ckv, cache_l0_mla_krope, cache_l0_swa_k, cache_l0_swa_v),
              (state_l1_gdn, cache_l1_attn_k, cache_l1_attn_v))

    x = x_prompt
    ctx_out = []
    for l in range(DEPTH):
        w_mod, b_mod, params = layer_params[l]
        mod = (jax.nn.silu(c_ctx) @ w_mod + b_mod)[None, None, :]
        x, new_ctx = layer_fns[l](x, mod, None, *params)
        ctx_out.extend(new_ctx)
    y_prompt = _rms(x, ln_f)

    x = x_sample
    for l in range(DEPTH):
        w_mod, b_mod, params = layer_params[l]
        mod = (jax.nn.silu(c) @ w_mod + b_mod)[:, None, :]
        x, _ = layer_fns[l](x, mod, caches[l], *params)
    y_sample = _rms(x, ln_f)

    l0_mla_ckv, l0_mla_krope, l0_swa_k, l0_swa_v, l1_gdn_state, l1_attn_k, l1_attn_v = ctx_out
    return (y_prompt, y_sample, l0_mla_ckv, l0_mla_krope, l0_swa_k, l0_swa_v, l1_gdn_state, l1_attn_k, l1_attn_v)
```

```python
import math
from contextlib import ExitStack
import numpy as np
import concourse.bass as bass
import concourse.mybir as mybir
from concourse.bass_utils import run_bass_kernel_spmd

F32 = mybir.dt.float32
BF16 = mybir.dt.bfloat16
AF = mybir.ActivationFunctionType
ALU = mybir.AluOpType
AX = mybir.AxisListType

D = 1024
EPS = 1e-6
L0C, L1C = 2464, 3600
MLA_SCALE = 96 ** -0.5
SWA_SCALE = 64 ** -0.5
ATT_SCALE = 128 ** -0.5
GDN_SCALE = 128 ** -0.5
DEBUG = False
LOOK = 3
DBG_OUT = {}


class TB:
    __slots__ = ("name", "h", "last_w", "readers", "excl")

    def __init__(self, name, h, excl=False):
        self.name, self.h, self.last_w, self.readers, self.excl = name, h, None, {}, excl

    def __getitem__(self, k):
        return self.h[k]


class Op:
    __slots__ = ("eng", "fn", "reads", "writes", "deps", "sig", "ev", "dma")

    def __init__(self, eng, fn, reads, writes, dma):
        self.eng, self.fn, self.reads, self.writes, self.dma = eng, fn, reads, writes, dma
        self.deps, self.sig, self.ev = (), False, None


class _Recorder:
    def __init__(self):
        self.calls = []

    def __getattr__(self, name):
        def f(*args, **kwargs):
            self.calls.append((name, args, kwargs))
        return f


class Prog:
    NDMA = 12

    def __init__(self, nc, stack):
        self.nc, self.stack, self.ops, self.n_sb = nc, stack, [], 0
        self.psum_banks, self.ps_i, self.tog, self.acc_i = [], 0, 0, 0
        self.cur, self.ps_slotted, self.ps_si, self.ps_n = 0, False, [0, 0], 4

    def sb(self, shape, dt=F32, name="t"):
        self.n_sb += 1
        h = self.stack.enter_context(self.nc.sbuf_tensor(f"{name}_{self.n_sb}", list(shape), dt))
        return TB(name, h)

    def ring(self, shape, dt, name, n=2, slots=1):
        bufs = [[self.sb(shape, dt, name) for _ in range(n)] for _ in range(slots)]
        st = [0] * slots

        def nxt():
            c = self.cur if slots > 1 else 0
            st[c] += 1
            return bufs[c][st[c] % n]
        return nxt

    def dram(self, name, shape, dt):
        return TB(name, self.nc.dram_tensor(name, list(shape), dt))

    def init_psum(self, n=8):
        for i in range(n):
            h = self.stack.enter_context(self.nc.psum_tensor(f"psb{i}", [128, 512], F32))
            self.psum_banks.append(TB(f"ps{i}", h, excl=True))

    def ps(self):
        if self.ps_slotted:
            self.ps_si[self.cur] += 1
            return self.psum_banks[4 * self.cur + self.ps_si[self.cur] % 4]
        t = self.psum_banks[self.ps_i % self.ps_n]
        self.ps_i += 1
        return t

    def ps_acc(self):
        t = self.psum_banks[4 + self.acc_i % 4]
        self.acc_i += 1
        return t

    def add(self, eng, fn, reads, writes, dma=False):
        rec = _Recorder()
        fn(rec)
        assert len(rec.calls) == 1, rec.calls
        name, args, kwargs = rec.calls[0]
        self.ops.append(Op(eng, lambda e: getattr(e, name)(*args, **kwargs), [r for r in reads if r is not None],
                           [w for w in writes if w is not None], dma))

    def pe(self, fn, r, w): self.add("pe", fn, r, w)
    def act(self, fn, r, w): self.add("act", fn, r, w)
    def dve(self, fn, r, w): self.add("dve", fn, r, w)
    def pool(self, fn, r, w): self.add("pool", fn, r, w)

    def any2(self, fn, r, w):
        self.tog += 1
        self.add("dve" if self.tog % 2 else "pool", fn, r, w)

    def dma(self, q, out, in_, reads, writes):
        self.add(q, lambda e: e.dma_start(out=out, in_=in_), reads, writes, dma=True)

    def mm(self, ps, out, lhsT, rhs, start, stop, reads):
        self.pe(lambda e: e.matmul(out, lhsT=lhsT, rhs=rhs, start=start, stop=stop), reads, [ps])

    def tr(self, ps, out, in_, ident, reads):
        self.pe(lambda e: e.transpose(out=out, in_=in_, identity=ident[:]), reads + [ident], [ps])

    def copy(self, out, in_, reads, writes):
        self.tog += 1
        if self.tog % 2:
            self.act(lambda e: e.copy(out=out, in_=in_), reads, writes)
        else:
            self.dve(lambda e: e.tensor_copy(out=out, in_=in_), reads, writes)

    def setup_sync(self):
        nc, st = self.nc, self.stack
        self.engs = {"pe": nc.tensor, "act": nc.scalar, "dve": nc.vector, "pool": nc.gpsimd, "sp": nc.sync}
        self.csem = {e: st.enter_context(nc.semaphore(f"c_{e}")) for e in ("pe", "act", "dve", "pool")}
        self.ccnt = {e: 0 for e in self.csem}
        self.dsem = {q: [st.enter_context(nc.semaphore(f"d_{q}{k}")) for k in range(self.NDMA)] for q in ("sp", "pool", "act")}
        self.dcnt = {q: [0] * self.NDMA for q in self.dsem}
        self.dnext = {q: 0 for q in self.dsem}
        self.waited = {}
        self.done = 0
        self.bar_ev = None

    def wait(self, eng, ev):
        s, v = ev
        key = (eng, id(s))
        if self.waited.get(key, 0) >= v:
            return
        self.waited[key] = v
        self.engs[eng].wait_ge(s, v)

    def flush(self):
        ops, start = self.ops, self.done
        for i in range(start, len(ops)):
            op = ops[i]
            deps = set()
            for t in op.reads:
                if t.last_w is not None:
                    deps.add(t.last_w)
                if t.excl:
                    deps.update(t.readers.values())
            for t in op.writes:
                if t.last_w is not None:
                    deps.add(t.last_w)
                deps.update(t.readers.values())
            deps.discard(i)
            deps = {d for d in deps if d >= start}
            if op.eng == "pe":
                deps = {d for d in deps if not (ops[d].eng == "pe" and not ops[d].dma)}
            op.deps = sorted(deps)
            for d in op.deps:
                ops[d].sig = True
            for t in op.writes:
                t.last_w, t.readers = i, {}
            for t in op.reads:
                if t.excl:
                    t.last_w, t.readers = i, {}
                else:
                    t.readers[("dma", i) if op.dma else op.eng] = i
        if self.bar_ev is not None:
            for e in ("pe", "act", "pool", "sp"):
                self.wait(e, self.bar_ev)
        for i in range(start, len(ops)):
            op = ops[i]
            e = op.eng
            for d in op.deps:
                self.wait(e, ops[d].ev)
            if op.dma:
                k = self.dnext[e] % self.NDMA
                self.dnext[e] += 1
                sm = self.dsem[e][k]
                if self.dcnt[e][k] > 0:
                    self.wait(e, (sm, self.dcnt[e][k]))
                ins = op.fn(self.engs[e])
                self.dcnt[e][k] += 16
                ins.then_inc(sm, 16)
                op.ev = (sm, self.dcnt[e][k])
            else:
                ins = op.fn(self.engs[e])
                if op.sig:
                    self.ccnt[e] += 1
                    ins.then_inc(self.csem[e], 1)
                    op.ev = (self.csem[e], self.ccnt[e])
        self.done = len(ops)
        for e in ("pe", "act", "pool"):
            self.ccnt[e] += 1
            self.engs[e].drain().then_inc(self.csem[e], 1)
            self.wait("dve", (self.csem[e], self.ccnt[e]))
        for q in self.dsem:
            for k in range(self.NDMA):
                if self.dcnt[q][k] > 0:
                    self.wait("dve", (self.dsem[q][k], self.dcnt[q][k]))
        self.ccnt["dve"] += 1
        self.engs["dve"].drain().then_inc(self.csem["dve"], 1)
        self.bar_ev = (self.csem["dve"], self.ccnt["dve"])

    def finish(self):
        for e in ("pe", "act", "pool", "sp"):
            self.wait(e, self.bar_ev)
        for q in ("sp", "pool"):
            for k in range(self.NDMA):
                if self.dcnt[q][k] > 0:
                    self.wait(q, (self.dsem[q][k], self.dcnt[q][k]))


def bc_last(tb, col0, n_outer, n_inner, pitch):
    return bass.AP(tb.h, col0, [[pitch, 128], [1, n_outer], [0, n_inner]])


def bc_mid(tb, col0, n_rep, n_inner, pitch):
    return bass.AP(tb.h, col0, [[pitch, 128], [0, n_rep], [1, n_inner]])


class Seq:
    pass


def build_program():
    nc = bass.Bass("TRN2", target_bir_lowering=False)

    def din(name, shape):
        return TB(name, nc.dram_tensor(name, list(shape), F32, kind="ExternalInput"))

    def dout(name, shape):
        return TB(name, nc.dram_tensor(name, list(shape), F32, kind="ExternalOutput"))

    I = {}
    for name, shape in [
        ("xp", (2, 256, D)), ("xs", (2048, D)), ("c_ckv", (512, 256)), ("c_kr", (512, 32)), ("c_sk", (512, 128)),
        ("c_sv", (512, 128)), ("c_gdn", (2, 4, 128, 128)), ("c_ak", (512, 256)), ("c_av", (512, 256)),
        ("cT", (128, 8, 2)), ("w_mod0", (D, 3 * D)), ("w_mod1", (D, 3 * D)), ("bm_fm", (128, 2, 16)),
        ("bm_row", (2, 2, D)), ("ln_fm", (128, 2, 8)), ("w_in0", (D, L0C)), ("w_in1", (D, L1C)), ("w_uq", (384, 768)),
        ("w_ukv", (256, 1024)), ("w_out0", (D, D)), ("w_out1", (D, D)), ("bc0", (128, 648)), ("bc1", (128, 400)),
        ("bcconv", (128, 4608)), ("bclnf", (128, D)), ("rt_mla", (2048, 2, 32)), ("rt_swa", (2048, 2, 64)),
        ("rt_att", (2048, 2, 128)), ("ident", (128, 128)), ("sel", (2, 2, 128)), ("swamask", (128, 6, 512)),
        ("gmask", (128, 8, 128)),
    ]:
        I[name] = din(name, shape)
    O = {}
    for name, shape in [
        ("y_p", (2, 256, D)), ("y_s", (2048, D)), ("o_ckv", (2, 256, 256)), ("o_kr", (2, 256, 32)),
        ("o_sk", (2, 256, 128)), ("o_sv", (2, 256, 128)), ("o_gdn", (2, 2, 4, 128, 128)), ("o_ak", (2, 256, 256)),
        ("o_av", (2, 256, 256)),
    ]:
        O[name] = dout(name, shape)

    with ExitStack() as st:
        P = Prog(nc, st)
        P.init_psum()
        P.setup_sync()
        R = {}

        idf = P.sb([128, 128], F32, "idf")
        idb = P.sb([128, 128], BF16, "idb")
        ones_bf = P.sb([128, 128], BF16, "ones")
        ones_f = P.sb([128, 128], F32, "onesf")
        P.dma("sp", idf[:], I["ident"][:], [], [idf])
        P.dma("pool", idb[:], I["ident"][:], [], [idb])
        P.dve(lambda e: e.memset(ones_bf[:], 1.0), [], [ones_bf])
        P.dve(lambda e: e.memset(ones_f[:], 1.0), [], [ones_f])
        sel = P.sb([2, 2, 128], F32, "sel")
        P.dma("sp", sel[:], I["sel"][:], [], [sel])
        bc0 = P.sb([128, 648], F32, "bc0")
        P.dma("sp", bc0[:], I["bc0"][:], [], [bc0])
        bc1 = P.sb([128, 400], F32, "bc1")
        P.dma("sp", bc1[:], I["bc1"][:], [], [bc1])
        esink = P.sb([128, 8], F32, "esink")
        P.act(lambda e: e.activation(out=esink[:], in_=bc0[:, 640:648], func=AF.Exp), [bc0], [esink])

        g_fm = [[P.sb([128, 8], F32, "gfm") for v in range(2)] for l in range(2)]
        s_fm = [[P.sb([128, 8], F32, "sfm") for v in range(2)] for l in range(2)]
        gate_bc = [[P.sb([128, D], F32, "gatebc") for v in range(2)] for l in range(2)]
        bclnf = P.sb([128, D], F32, "bclnf")
        P.dma("sp", bclnf[:], I["bclnf"][:], [], [bclnf])
        gb_all = [P.sb([128, T_ // 128, 16], F32, "gb") for T_ in (256, 256, 2048)]
        x_ring = xn_ring = hT_ring = rt_ring = rp_ring = None
        junk_ring = st_ring = st8_ring = stg_ring = tm_ring = tb_ring = None

        def alloc_stage(full=True, need_x=True):
            nonlocal x_ring, xn_ring, hT_ring, rt_ring, rp_ring, junk_ring, st_ring, st8_ring, stg_ring, tm_ring, tb_ring
            ns = 2 if full else 1
            if need_x:
                x_ring = P.ring([128, D], F32, "xt", 2, ns)
            junk_ring = P.ring([128, D], BF16, "junk", 1 if full else 2, ns)
            st_ring = P.ring([128, 1], F32, "stat", 8, ns)
            st8_ring = P.ring([128, 8], F32, "stat8", 8, ns)
            stg_ring = P.ring([128, 8, 128], BF16, "stg", 4, ns)
            tm_ring = P.ring([128, 512], F32, "tm32", 3 if full else 4, ns)
            tb_ring = P.ring([128, 768], BF16, "tmb", 4, ns)
            if full:
                xn_ring = P.ring([128, D], BF16, "xn", 1, ns)
                hT_ring = P.ring([128, 8, 128], BF16, "hT", 1, ns)
                rt_ring = P.ring([128, 2, 128], F32, "rt", 4, ns)
                rp_ring = P.ring([128, 512], F32, "rp32", 2, ns)

        def run_multi(jobs):
            G = [(j, i) for j, jb in enumerate(jobs) for i in range(jb[0])]
            run_slotted(len(G), lambda g: jobs[G[g][0]][1](G[g][1]), lambda g, loaded: jobs[G[g][0]][2](G[g][1], loaded))
            for jb in jobs:
                jb[3]()

        def run_slotted(NT, load_tile, body, stag=6):
            P.ps_slotted = True
            pre = {}

            def ld(slot, i):
                if i < NT and i not in pre:
                    P.cur = slot
                    pre[i] = load_tile(i)
            gens, nexti = [None, None], [0, 1]
            ld(0, 0)
            ld(1, 1)
            rnd = 0
            while True:
                busy = False
                for slot in (0, 1):
                    if gens[slot] is None and nexti[slot] < NT and not (slot == 1 and rnd < stag):
                        i = nexti[slot]
                        nexti[slot] += 2
                        ld(slot, i + 2)
                        P.cur = slot
                        gens[slot] = body(i, pre.pop(i))
                    if gens[slot] is not None:
                        busy = True
                        P.cur = slot
                        try:
                            next(gens[slot])
                        except StopIteration:
                            gens[slot] = None
                rnd += 1
                if not busy and all(n >= NT for n in nexti) and rnd > stag:
                    break
            P.cur, P.ps_slotted = 0, False

        ph = ExitStack()
        P.stack = ph
        cT = P.sb([128, 8, 2], F32, "cT")
        P.dma("sp", cT[:], I["cT"][:], [], [cT])
        scT = P.sb([128, 8, 2], BF16, "scT")
        P.act(lambda e: e.activation(out=scT[:], in_=cT[:], func=AF.Silu), [cT], [scT])
        bmfm = P.sb([128, 2, 16], F32, "bmfm")
        P.dma("sp", bmfm[:], I["bm_fm"][:], [], [bmfm])
        bmrow = P.sb([2, 2, D], F32, "bmrow")
        P.dma("sp", bmrow[:], I["bm_row"][:], [], [bmrow])
        lnfm = P.sb([128, 2, 8], F32, "lnfm")
        P.dma("sp", lnfm[:], I["ln_fm"][:], [], [lnfm])
        wm_ring = P.ring([128, 8, 512], BF16, "wm", 2)
        modfm = P.sb([128, 2, 16], F32, "modfm")
        grow = P.sb([2, D], F32, "grow")
        for l in range(2):
            wsrc = I["w_mod0" if l == 0 else "w_mod1"]
            psm = P.ps_acc()
            for blk in range(6):
                wm = wm_ring()
                P.dma("pool", wm[:], wsrc.h.rearrange("(c p) n -> p c n", p=128)[:, :, blk * 512:(blk + 1) * 512], [], [wm])
                if blk < 4:
                    for j in range(4):
                        o = (blk * 4 + j) * 2
                        for ch in range(8):
                            P.mm(psm, psm[:, o:o + 2], wm[:, ch, j * 128:(j + 1) * 128], scT[:, ch, :], ch == 0, ch == 7, [wm, scT])
                else:
                    psg = P.ps()
                    for ch in range(8):
                        P.mm(psg, psg[0:2, :], scT[:, ch, :], wm[:, ch, :], ch == 0, ch == 7, [wm, scT])
                    hf = blk - 4
                    P.dve(lambda e, psg=psg, hf=hf, l=l: e.tensor_tensor(out=grow[0:2, hf * 512:(hf + 1) * 512], in0=psg[0:2, :],
                                                                       in1=bmrow[0:2, l, hf * 512:(hf + 1) * 512], op=ALU.add), [psg, bmrow], [grow])
            for v in range(2):
                P.dve(lambda e, v=v, l=l, psm=psm: e.tensor_tensor(out=modfm[:, v, :], in0=psm[:, 0:32].rearrange("p (j v) -> p v j", v=2)[:, v, :], in1=bmfm[:, l, :], op=ALU.add), [psm, bmfm], [modfm])
                P.dve(lambda e, v=v, l=l: e.scalar_tensor_tensor(out=g_fm[l][v][:], in0=modfm[:, v, 8:16], scalar=1.0, in1=lnfm[:, l, :],
                                                                 op0=ALU.add, op1=ALU.mult), [modfm, lnfm], [g_fm[l][v]])
                P.dve(lambda e, v=v, l=l: e.tensor_copy(out=s_fm[l][v][:], in_=modfm[:, v, 0:8]), [modfm], [s_fm[l][v]])
                for hf in range(2):
                    psb = P.ps()
                    P.mm(psb, psb[:, :], sel[0:2, v, :], grow[0:2, hf * 512:(hf + 1) * 512], True, True, [sel, grow])
                    P.copy(gate_bc[l][v][:, hf * 512:(hf + 1) * 512], psb[:, :], [psb], [gate_bc[l][v]])

        P.flush()
        ph.close()
        P.stack = st
        seqs = []
        for si in range(3):
            s = Seq()
            s.i, s.prompt = si, si < 2
            s.T = 256 if s.prompt else 2048
            s.Tk = s.T + (0 if s.prompt else 512)
            s.v = 0 if s.prompt else 1
            s.x0 = (lambda r0, si=si: I["xp"].h[si, r0:r0 + 128, :]) if s.prompt else (lambda r0: I["xs"].h[r0:r0 + 128, :])
            s.x0tb = I["xp"] if s.prompt else I["xs"]
            n = f"s{si}"
            T, Tk = s.T, s.Tk
            if DEBUG and not s.prompt:
                s.X1 = TB("dbg_x1", nc.dram_tensor("dbg_x1", [T, D], F32, kind="ExternalOutput"))
            else:
                s.X1 = P.dram(n + "X1", [T, D], F32)
            if DEBUG and not s.prompt:
                s.QA = TB("dbg_qa", nc.dram_tensor("dbg_qa", [768, T], BF16, kind="ExternalOutput"))
                s.KA = TB("dbg_ka", nc.dram_tensor("dbg_ka", [768, Tk], BF16, kind="ExternalOutput"))
                s.VA = TB("dbg_va", nc.dram_tensor("dbg_va", [Tk, 512], BF16, kind="ExternalOutput"))
                s.ZT0 = TB("dbg_zt0", nc.dram_tensor("dbg_zt0", [D, T], BF16, kind="ExternalOutput"))
            else:
                s.QA = P.dram(n + "QA", [768, T], BF16)
                s.KA = P.dram(n + "KA", [768, Tk], BF16)
                s.VA = P.dram(n + "VA", [Tk, 512], BF16)
                s.ZT0 = P.dram(n + "ZT0", [D, T], BF16)
            s.QB = P.dram(n + "QB", [512, T], BF16)
            s.KB = P.dram(n + "KB", [128, Tk], BF16)
            s.VB = P.dram(n + "VB", [Tk, 128], BF16)
            if DEBUG and not s.prompt:
                s.OG0 = TB("dbg_og0", nc.dram_tensor("dbg_og0", [D, T], BF16, kind="ExternalOutput"))
            else:
                s.OG0 = P.dram(n + "OG0", [D, T], BF16)
            s.QD = P.dram(n + "QD", [512, T], BF16)
            s.KD = P.dram(n + "KD", [256, Tk], BF16)
            s.VD = P.dram(n + "VD", [Tk, 256], BF16)
            s.ZT1 = P.dram(n + "ZT1", [D, T], BF16)
            s.OG1 = P.dram(n + "OG1", [D, T], BF16)
            s.QKV = P.dram(n + "QKV", [T + 2, 1536], F32)
            s.ZC = P.dram(n + "ZC", [T, 512], F32)
            s.OF = P.dram(n + "OF", [T, 512], F32)
            seqs.append(s)

        def build_hT_g(xt, l, v):
            junk, ss, xn, hT = junk_ring(), st_ring(), xn_ring(), hT_ring()
            P.dve(lambda e: e.memset(ss[:], 0.0), [], [ss])
            P.act(lambda e: e.activation(out=junk[:], in_=xt[:], func=AF.Square, accum_out=ss[:]), [xt], [junk, ss])
            P.act(lambda e: e.activation(out=ss[:], in_=ss[:], func=AF.Ln, scale=1.0 / D, bias=EPS), [ss], [ss])
            P.act(lambda e: e.activation(out=ss[:], in_=ss[:], func=AF.Exp, scale=-0.5), [ss], [ss])
            P.act(lambda e: e.activation(out=xn[:], in_=xt[:], func=AF.Copy, scale=ss[:]), [xt, ss], [xn])
            yield
            pt = P.ps()
            ptb = pt.h[:].bitcast(BF16)
            for c in range(8):
                P.tr(pt, ptb[:, c * 128:(c + 1) * 128], xn[:, c * 128:(c + 1) * 128], idb, [xn])
            yield
            g, s_ = g_fm[l][v], s_fm[l][v]
            for c in range(8):
                if c % 2 == 0:
                    P.act(lambda e, c=c: e.activation(out=hT[:, c, :], in_=ptb[:, c * 128:(c + 1) * 128], func=AF.Identity,
                                                      scale=g[:, c:c + 1], bias=s_[:, c:c + 1]), [pt, g, s_], [hT])
                else:
                    P.dve(lambda e, c=c: e.tensor_scalar(out=hT[:, c, :], in0=ptb[:, c * 128:(c + 1) * 128], scalar1=g[:, c:c + 1],
                                                         scalar2=s_[:, c:c + 1], op0=ALU.mult, op1=ALU.add), [pt, g, s_], [hT])
            return hT

        def proj(hT, W, col0, ncols):
            ps = P.ps()
            for c in range(8):
                P.mm(ps, ps[:, 0:ncols], hT[:, c, :], W[:, c, col0:col0 + ncols], c == 0, c == 7, [hT, W])
            return ps

        def rms_stat(src_ap, src_tb, n, nh=1):
            junk = junk_ring()
            if nh == 1:
                ss = st_ring()
                P.dve(lambda e: e.memset(ss[:], 0.0), [], [ss])
                P.act(lambda e: e.activation(out=junk[:, 0:n], in_=src_ap, func=AF.Square, accum_out=ss[:]), [src_tb], [junk, ss])
                sv = ss[:]
            else:
                ss = st8_ring()
                P.act(lambda e: e.activation(out=junk[:, 0:nh * n], in_=src_ap, func=AF.Square), [src_tb], [junk])
                P.dve(lambda e: e.tensor_reduce(out=ss[:, 0:nh], in_=junk[:, 0:nh * n].rearrange("p (h d) -> p h d", h=nh), axis=AX.X, op=ALU.add), [junk], [ss])
                sv = ss[:, 0:nh]
            P.act(lambda e: e.activation(out=sv, in_=sv, func=AF.Ln, scale=1.0 / n, bias=EPS), [ss], [ss])
            P.act(lambda e: e.activation(out=sv, in_=sv, func=AF.Exp, scale=-0.5), [ss], [ss])
            return ss

        def rope(src_ap3, src_tb, out_ap3, out_tb, rt, H, R):
            q = R // 4
            t1, t2 = rp_ring(), rp_ring()
            t1v = t1[:, 0:H * R].rearrange("p (h r) -> p h r", h=H)
            t2v = t2[:, 0:H * R].rearrange("p (h r) -> p h r", h=H)
            P.dve(lambda e: e.tensor_tensor(out=t1v, in0=src_ap3, in1=bc_mid(rt, 0, H, R, 256), op=ALU.mult), [src_tb, rt], [t1])
            for hf in range(2):
                for j in range(2):
                    a = hf * 2 * q + j * q
                    b = hf * 2 * q + (1 - j) * q
                    P.dve(lambda e, a=a, b=b: e.tensor_tensor(out=t2v[:, :, a:a + q], in0=src_ap3[:, :, b:b + q],
                                                              in1=bc_mid(rt, 128 + a, H, q, 256), op=ALU.mult), [src_tb, rt], [t2])
            P.pool(lambda e: e.tensor_tensor(out=out_ap3, in0=t1v, in1=t2v, op=ALU.add), [t1, t2], [out_tb])

        def tr_store(items, dst_ap, dst_tb, nrows):
            for _ in tr_store_g(items, dst_ap, dst_tb, nrows):
                pass

        def tr_store_g(items, dst_ap, dst_tb, nrows):
            pt = P.ps()
            ptb = pt.h[:].bitcast(BF16)
            for k, (ap, tb) in enumerate(items):
                P.tr(pt, ptb[0:nrows, k * 128:(k + 1) * 128], ap, idb, [tb])
            yield
            stg = stg_ring()
            n = len(items)
            P.copy(stg[0:nrows, 0:n, :], ptb[0:nrows, 0:n * 128].rearrange("p (k t) -> p k t", k=n), [pt], [stg])
            P.dma("sp", dst_ap, stg[0:nrows, 0:n, :], [stg], [dst_tb])

        ph = ExitStack()
        P.stack = ph
        alloc_stage()
        w_in0 = P.sb([128, 8, L0C], BF16, "w_in0")
        for c in range(8):
            P.dma("pool", w_in0[:, c, :], I["w_in0"].h[c * 128:(c + 1) * 128, :], [], [w_in0])
        w_uq = P.sb([128, 3, 768], BF16, "w_uq")
        P.dma("pool", w_uq[:], I["w_uq"].h.rearrange("(c p) n -> p c n", p=128), [], [w_uq])
        w_ukv = P.sb([128, 2, 1024], BF16, "w_ukv")
        P.dma("pool", w_ukv[:], I["w_ukv"].h.rearrange("(c p) n -> p c n", p=128), [], [w_ukv])

        def kside0(*a):
            for _ in kside0_g(*a):
                pass

        def kside0_g(s, col, ckvn, ckvn_tb, kr_ap, kr_tb, kb_ap, kb_tb, vb_ap, vb_tb):
            pt = P.ps()
            ptb = pt.h[:].bitcast(BF16)
            for c in range(2):
                P.tr(pt, ptb[:, c * 128:(c + 1) * 128], ckvn[:, c * 128:(c + 1) * 128], idb, [ckvn_tb])
            yield
            ckT = stg_ring()
            P.copy(ckT[:, 0:2, :], ptb[:, 0:256].rearrange("p (k t) -> p k t", k=2), [pt], [ckT])
            yield
            psk, psv = P.ps(), P.ps()
            for c in range(2):
                P.mm(psk, psk[:, :], ckT[:, c, :], w_ukv[:, c, 0:512], c == 0, c == 1, [ckT, w_ukv])
            for c in range(2):
                P.mm(psv, psv[:, :], ckT[:, c, :], w_ukv[:, c, 512:1024], c == 0, c == 1, [ckT, w_ukv])
            yield
            ka = tb_ring()
            kav = ka[:, 0:768].rearrange("p (h d) -> p h d", h=8)
            P.act(lambda e: e.copy(out=kav[:, :, 0:64], in_=psk[:, :].rearrange("p (h d) -> p h d", h=8)), [psk], [ka])
            krt = tb_ring()
            P.dve(lambda e: e.tensor_copy(out=krt[:, 0:32], in_=kr_ap), [kr_tb], [krt])
            P.dve(lambda e: e.tensor_copy(out=kav[:, :, 64:96], in_=bc_mid(krt, 0, 8, 32, 768)), [krt], [ka])
            va = tb_ring()
            P.dve(lambda e: e.tensor_copy(out=va[:, 0:512], in_=psv[:, :]), [psv], [va])
            P.dma("sp", s.VA.h[col:col + 128, :], va[:, 0:512], [va], [s.VA])
            yield
            yield from tr_store_g([(kav[:, h, :], ka) for h in range(8)], s.KA.h[:, col:col + 128].rearrange("(h d) t -> d h t", h=8), s.KA, 96)
            yield
            yield from tr_store_g([(kb_ap, kb_tb)], s.KB.h[:, col:col + 128].rearrange("(k p) t -> p k t", k=1), s.KB, 128)
            P.dma("sp", s.VB.h[col:col + 128, :], vb_ap, [vb_tb], [s.VB])

        def l0_stage1(s):
            T = s.T

            def load_tile(i):
                r0 = i * 128
                xt = x_ring()
                P.dma("sp", xt[:], s.x0(r0), [], [xt])
                rtm = rts = None
                if not s.prompt:
                    rtm, rts = rt_ring(), rt_ring()
                    P.dma("sp", rtm[:, :, 0:32], I["rt_mla"].h[r0:r0 + 128, :, :], [], [rtm])
                    P.dma("sp", rts[:, :, 0:64], I["rt_swa"].h[r0:r0 + 128, :, :], [], [rts])
                return xt, rtm, rts
            def body(i, loaded):
                r0 = i * 128
                xt, rtm, rts = loaded
                hT = yield from build_hT_g(xt, 0, s.v)
                yield
                psa = proj(hT, w_in0, 0, 384)
                yield
                rs = rms_stat(psa[:, 0:384], psa, 384)
                yield
                cqn = tb_ring()
                P.dve(lambda e, psa=psa, rs=rs, cqn=cqn: e.scalar_tensor_tensor(out=cqn[:, 0:384], in0=psa[:, 0:384], scalar=rs[:], in1=bc0[:, 0:384],
                                                                                op0=ALU.mult, op1=ALU.mult), [psa, rs, bc0], [cqn])
                pt = P.ps()
                ptb = pt.h[:].bitcast(BF16)
                for c in range(3):
                    P.tr(pt, ptb[:, c * 128:(c + 1) * 128], cqn[:, c * 128:(c + 1) * 128], idb, [cqn])
                cqT = stg_ring()
                P.copy(cqT[:, 0:3, :], ptb[:, 0:384].rearrange("p (k t) -> p k t", k=3), [pt], [cqT])
                qa = tb_ring()
                qav = qa[:, 0:768].rearrange("p (h d) -> p h d", h=8)
                for g2 in range(2):
                    psq = P.ps()
                    for c in range(3):
                        P.mm(psq, psq[:, 0:384], cqT[:, c, :], w_uq[:, c, g2 * 384:(g2 + 1) * 384], c == 0, c == 2, [cqT, w_uq])
                    pv = psq[:, 0:384].rearrange("p (h d) -> p h d", h=4)
                    if s.prompt:
                        P.copy(qav[:, g2 * 4:(g2 + 1) * 4, :], pv, [psq], [qa])
                    else:
                        P.act(lambda e, pv=pv, g2=g2: e.copy(out=qav[:, g2 * 4:(g2 + 1) * 4, 0:64], in_=pv[:, :, 0:64]), [psq], [qa])
                        rope(pv[:, :, 64:96], psq, qav[:, g2 * 4:(g2 + 1) * 4, 64:96], qa, rtm, 4, 32)
                yield
                yield from tr_store_g([(qav[:, h, :], qa) for h in range(8)], s.QA.h[:, r0:r0 + 128].rearrange("(h d) t -> d h t", h=8), s.QA, 96)
                yield
                psb_ = proj(hT, w_in0, 384, 288)
                yield
                rs = rms_stat(psb_[:, 0:256], psb_, 256)
                yield
                ck32 = tm_ring()
                P.dve(lambda e, psb_=psb_, rs=rs, ck32=ck32: e.scalar_tensor_tensor(out=ck32[:, 0:256], in0=psb_[:, 0:256], scalar=rs[:], in1=bc0[:, 384:640],
                                                                                    op0=ALU.mult, op1=ALU.mult), [psb_, rs, bc0], [ck32])
                kr32 = tm_ring()
                if s.prompt:
                    P.act(lambda e, psb_=psb_, kr32=kr32: e.copy(out=kr32[:, 0:32], in_=psb_[:, 256:288]), [psb_], [kr32])
                    P.dma("sp", O["o_ckv"].h[s.i, r0:r0 + 128, :], ck32[:, 0:256], [ck32], [O["o_ckv"]])
                    P.dma("sp", O["o_kr"].h[s.i, r0:r0 + 128, :], kr32[:, 0:32], [kr32], [O["o_kr"]])
                else:
                    rope(psb_[:, 256:288].rearrange("p (h r) -> p h r", h=1), psb_, kr32[:, 0:32].rearrange("p (h r) -> p h r", h=1), kr32, rtm, 1, 32)
                ckn = tb_ring()
                P.pool(lambda e, ckn=ckn, ck32=ck32: e.tensor_copy(out=ckn[:, 0:256], in_=ck32[:, 0:256]), [ck32], [ckn])
                yield
                pse = proj(hT, w_in0, 1696, 256)
                yield
                kb = tb_ring()
                if s.prompt:
                    kv32 = tm_ring()
                    P.copy(kv32[:, 0:256], pse[:, 0:256], [pse], [kv32])
                    P.dma("sp", O["o_sk"].h[s.i, r0:r0 + 128, :], kv32[:, 0:128], [kv32], [O["o_sk"]])
                    P.dma("sp", O["o_sv"].h[s.i, r0:r0 + 128, :], kv32[:, 128:256], [kv32], [O["o_sv"]])
                    P.pool(lambda e, kb=kb, kv32=kv32: e.tensor_copy(out=kb[:, 0:256], in_=kv32[:, 0:256]), [kv32], [kb])
                else:
                    rope(pse[:, 0:128].rearrange("p (h r) -> p h r", h=2), pse, kb[:, 0:128].rearrange("p (h r) -> p h r", h=2), kb, rts, 2, 64)
                    P.act(lambda e, kb=kb, pse=pse: e.copy(out=kb[:, 128:256], in_=pse[:, 128:256]), [pse], [kb])
                yield
                yield from kside0_g(s, r0, ckn, ckn, kr32[:, 0:32], kr32, kb[:, 0:128], kb, kb[:, 128:256], kb)
                yield
                for zi, col0 in enumerate((672, 1952)):
                    psz = proj(hT, w_in0, col0, 512)
                    yield
                    zs = tb_ring()
                    P.act(lambda e, psz=psz, zs=zs: e.activation(out=zs[:, 0:512], in_=psz[:, :], func=AF.Silu), [psz], [zs])
                    yield
                    yield from tr_store_g([(zs[:, k * 128:(k + 1) * 128], zs) for k in range(4)],
                             s.ZT0.h[zi * 512:(zi + 1) * 512, r0:r0 + 128].rearrange("(k p) t -> p k t", p=128), s.ZT0, 128)
                    yield
                yield
                psd = proj(hT, w_in0, 1184, 512)
                yield
                qb = tb_ring()
                if s.prompt:
                    P.copy(qb[:, 0:512], psd[:, :], [psd], [qb])
                else:
                    rope(psd[:, :].rearrange("p (h r) -> p h r", h=8), psd, qb[:, 0:512].rearrange("p (h r) -> p h r", h=8), qb, rts, 8, 64)
                yield
                yield from tr_store_g([(qb[:, k * 128:(k + 1) * 128], qb) for k in range(4)],
                         s.QB.h[:, r0:r0 + 128].rearrange("(k p) t -> p k t", p=128), s.QB, 128)
            def post():
              if not s.prompt:
                for j in range(4):
                    c0 = j * 128
                    ckn, krb, kbb = tb_ring(), tb_ring(), tb_ring()
                    P.dma("pool", ckn[:, 0:256], I["c_ckv"].h[c0:c0 + 128, :], [], [ckn])
                    P.dma("pool", krb[:, 0:32], I["c_kr"].h[c0:c0 + 128, :], [], [krb])
                    P.dma("pool", kbb[:, 0:128], I["c_sk"].h[c0:c0 + 128, :], [], [kbb])
                    P.dma("pool", kbb[:, 128:256], I["c_sv"].h[c0:c0 + 128, :], [], [kbb])
                    kside0(s, s.T + c0, ckn, ckn, krb[:, 0:32], krb, kbb[:, 0:128], kbb, kbb[:, 128:256], kbb)
            return (T // 128, load_tile, body, post)

        RSD = {"i": 0, "t": [P.dram(f"rsd{j}", [1, 512], F32) for j in range(4)]}

        def alloc_att():
            R["kT"] = P.ring([128, 2560], BF16, "kT", 3)
            R["v"] = P.ring([128, 20, 128], BF16, "vt", 3)
            for _ in range(3):
                vb_ = R["v"]()
                P.dve(lambda e, vb_=vb_: e.memset(vb_[:, :, 64:128], 0.0), [], [vb_])
                P.dve(lambda e, vb_=vb_: e.memset(vb_[:, :, 64:65], 1.0), [], [vb_])
            R["q"] = P.ring([128, 512], BF16, "qT", 3)
            R["z"] = P.ring([128, 512], BF16, "zs", 3)
            R["p"] = P.ring([128, 512], BF16, "pT", 5)
            R["r"] = P.ring([128, 512], F32, "rs", 2)
            R["t"] = P.ring([128, 512], F32, "tt", 2)
            R["rbc"] = P.ring([128, 512], F32, "rbc", 2)
            R["og"] = P.ring([128, 512], BF16, "og", 2)
            R["swamask"] = P.sb([128, 6, 512], BF16, "swamask")
            P.dma("pool", R["swamask"][:], I["swamask"][:], [], [R["swamask"]])

        def alloc_out():
            alloc_stage(True)
            R["ogin"] = P.ring([128, 8, 128], BF16, "ogin", 2, 2)
            R["xo"] = P.ring([128, D], F32, "xo", 2, 2)
            R["w_out"] = P.sb([128, 8, D], BF16, "w_out")

        def attend(s, QT, KT, V, ZT, OG, nh, kvmap, Kd, dv, scale, zrow, ogrow, swa, sink):
            T, Tk = s.T, s.Tk
            QBK = min(512, T)
            nqb, nkt = T // QBK, Tk // 128
            swamask = R["swamask"]
            blocks = [(h, qb) for h in range(nh) for qb in range(nqb)]
            heads, pre = {}, {}

            ensured = set()

            def ensure(nb):
                if nb >= len(blocks) or nb in ensured:
                    return
                ensured.add(nb)
                h, qb = blocks[nb]
                if h not in heads:
                    kv = kvmap(h)
                    kT, vt = R["kT"](), R["v"]()
                    P.dma("sp", kT[0:Kd, 0:Tk], KT.h[kv * Kd:(kv + 1) * Kd, 0:Tk], [KT], [kT])
                    P.dma("sp", vt[:, 0:nkt, 0:dv], V.h[0:Tk, kv * dv:(kv + 1) * dv].rearrange("(n p) d -> p n d", p=128), [V], [vt])
                    heads[h] = (kT, vt)
                q0 = qb * QBK
                qT, zs = R["q"](), R["z"]()
                P.dma("sp", qT[0:Kd, 0:QBK], QT.h[h * Kd:(h + 1) * Kd, q0:q0 + QBK], [QT], [qT])
                P.dma("sp", zs[0:dv, 0:QBK], ZT.h[zrow(h):zrow(h) + dv, q0:q0 + QBK], [ZT], [zs])
                pre[nb] = (qT, zs)
            KK = Kd
            if Kd == 64:
                KK = 128
                for _ in range(3):
                    kb_, qb2_ = R["kT"](), R["q"]()
                    P.dve(lambda e, kb_=kb_: e.memset(kb_[64:128, :], 0.0), [], [kb_])
                    P.dve(lambda e, qb2_=qb2_: e.memset(qb2_[64:128, :], 0.0), [], [qb2_])

            def tiles_of(qb_):
                if swa and not s.prompt:
                    tl = [(kt, kt - 4 * qb_ + 1) for kt in range(4 * qb_ - 1, 4 * qb_ + 5) if 0 <= kt < T // 128]
                    return tl + [(kt, None) for kt in range(T // 128, nkt)]
                return [(kt, None) for kt in range(nkt)]

            def issue_qk_for(nb_, j, pend_):
                h_, qb_ = blocks[nb_]
                kT_, qT_ = heads[h_][0], pre[nb_][0]
                ps = P.ps()
                kt_ = tiles_of(qb_)[j][0]
                P.mm(ps, ps[:, 0:QBK], kT_[0:KK, kt_ * 128:(kt_ + 1) * 128], qT_[0:KK, 0:QBK], True, True, [kT_, qT_])
                pend_[j] = ps
            ensure(0)
            ensure(1)
            pend_next = {}
            for j in range(min(LOOK, len(tiles_of(blocks[0][1])))):
                issue_qk_for(0, j, pend_next)
            for nb, (h, qb) in enumerate(blocks):
                ensure(nb + 1)
                ensure(nb + 2)
                kT, vt = heads[h]
                if True:
                    q0 = qb * QBK
                    qT, zs = pre[nb]
                    tiles = tiles_of(qb)
                    aug = dv == 64
                    M = dv + 1 if aug else dv
                    pso = P.ps_acc()
                    pss = None if aug else P.ps_acc()
                    n = len(tiles)
                    pend = pend_next

                    def issue_qk(j, nb=nb, pend=pend):
                        issue_qk_for(nb, j, pend)
                    for idx, (kt, mk) in enumerate(tiles):
                        ps = pend.pop(idx)
                        pt = R["p"]()
                        P.act(lambda e, ps=ps, pt=pt: e.activation(out=pt[:, 0:QBK], in_=ps[:, 0:QBK], func=AF.Exp, scale=scale), [ps], [pt])
                        if mk is not None:
                            P.dve(lambda e, pt=pt, mk=mk: e.tensor_tensor(out=pt[:, 0:QBK], in0=pt[:, 0:QBK], in1=swamask[:, mk, 0:QBK], op=ALU.mult), [pt, swamask], [pt])
                        if idx + LOOK < n:
                            issue_qk(idx + LOOK)
                        a, b = idx == 0, idx == n - 1
                        MM_ = 128 if aug else M
                        P.mm(pso, pso[0:MM_, 0:QBK], vt[:, kt, 0:MM_], pt[:, 0:QBK], a, b, [vt, pt])
                        if not aug:
                            P.mm(pss, pss[0:dv, 0:QBK], ones_bf[:, 0:dv], pt[:, 0:QBK], a, b, [ones_bf, pt])
                    pend_next = {}
                    if nb + 1 < len(blocks):
                        for j in range(min(LOOK, len(tiles_of(blocks[nb + 1][1])))):
                            issue_qk_for(nb + 1, j, pend_next)
                    pre.pop(nb)
                    rs, tt, og = R["r"](), R["t"](), R["og"]()
                    if aug:
                        r1 = slice(dv, dv + 1)
                        if sink:
                            P.act(lambda e, rs=rs, pso=pso, h=h: e.activation(out=rs[r1, 0:QBK], in_=pso[r1, 0:QBK], func=AF.Ln, bias=esink[r1, h:h + 1]), [pso, esink], [rs])
                            P.act(lambda e, rs=rs: e.activation(out=rs[r1, 0:QBK], in_=rs[r1, 0:QBK], func=AF.Exp, scale=-1.0), [rs], [rs])
                        else:
                            P.dve(lambda e, rs=rs, pso=pso: e.reciprocal(out=rs[r1, 0:QBK], in_=pso[r1, 0:QBK]), [pso], [rs])
                        rbc = R["rbc"]()
                        RSD["i"] += 1
                        dsc = RSD["t"][RSD["i"] % 4]
                        P.dma("sp", dsc.h[0:1, 0:QBK], rs[r1, 0:QBK], [rs], [dsc])
                        P.dma("sp", rbc[0:dv, 0:QBK], bass.AP(dsc.h, 0, [[0, dv], [1, QBK]]), [dsc], [rbc])
                        P.dve(lambda e, tt=tt, pso=pso, zs=zs: e.tensor_tensor(out=tt[0:dv, 0:QBK], in0=pso[0:dv, 0:QBK], in1=zs[0:dv, 0:QBK], op=ALU.mult), [pso, zs], [tt])
                        P.pool(lambda e, og=og, tt=tt, rbc=rbc: e.tensor_tensor(out=og[0:dv, 0:QBK], in0=tt[0:dv, 0:QBK], in1=rbc[0:dv, 0:QBK], op=ALU.mult), [tt, rbc], [og])
                    else:
                        P.dve(lambda e, rs=rs, pss=pss: e.reciprocal(out=rs[0:dv, 0:QBK], in_=pss[0:dv, 0:QBK]), [pss], [rs])
                        P.dve(lambda e, tt=tt, pso=pso, zs=zs: e.tensor_tensor(out=tt[0:dv, 0:QBK], in0=pso[0:dv, 0:QBK], in1=zs[0:dv, 0:QBK], op=ALU.mult), [pso, zs], [tt])
                        P.pool(lambda e, og=og, tt=tt, rs=rs: e.tensor_tensor(out=og[0:dv, 0:QBK], in0=tt[0:dv, 0:QBK], in1=rs[0:dv, 0:QBK], op=ALU.mult), [tt, rs], [og])
                    P.dma("sp", OG.h[ogrow(h):ogrow(h) + dv, q0:q0 + QBK], og[0:dv, 0:QBK], [og], [OG])


        def outproj(s, l, OG, w_out, xsrc, xsrc_tb, Xdst, final):
            def load_tile(i):
                r0 = i * 128
                og, xt = R["ogin"](), x_ring()
                P.dma("sp", og[:], OG.h[:, r0:r0 + 128].rearrange("(c p) t -> p c t", p=128), [OG], [og])
                P.dma("sp", xt[:], xsrc(r0), [xsrc_tb] if xsrc_tb is not None else [], [xt])
                return og, xt
            def body(i, loaded):
                r0 = i * 128
                og, xt = loaded
                xo = R["xo"]()
                for hf in range(2):
                    ps = P.ps()
                    for c in range(8):
                        P.mm(ps, ps[:, :], og[:, c, :], w_out[:, c, hf * 512:(hf + 1) * 512], c == 0, c == 7, [og, w_out])
                    yield
                    tt = tm_ring()
                    P.dve(lambda e, tt=tt, ps=ps, hf=hf: e.tensor_tensor(out=tt[:, :], in0=ps[:, :], in1=gate_bc[l][s.v][:, hf * 512:(hf + 1) * 512], op=ALU.mult),
                          [ps, gate_bc[l][s.v]], [tt])
                    yield
                    P.pool(lambda e, tt=tt, xo=xo, xt=xt, hf=hf: e.tensor_tensor(out=xo[:, hf * 512:(hf + 1) * 512], in0=tt[:, :], in1=xt[:, hf * 512:(hf + 1) * 512], op=ALU.add),
                           [tt, xt], [xo])
                    yield
                if not final:
                    P.dma("sp", Xdst.h[r0:r0 + 128, :], xo[:], [xo], [Xdst])
                else:
                    junk, ss = junk_ring(), st_ring()
                    P.dve(lambda e, ss=ss: e.memset(ss[:], 0.0), [], [ss])
                    P.act(lambda e, junk=junk, xo=xo, ss=ss: e.activation(out=junk[:], in_=xo[:], func=AF.Square, accum_out=ss[:]), [xo], [junk, ss])
                    yield
                    P.act(lambda e, ss=ss: e.activation(out=ss[:], in_=ss[:], func=AF.Ln, scale=1.0 / D, bias=EPS), [ss], [ss])
                    P.act(lambda e, ss=ss: e.activation(out=ss[:], in_=ss[:], func=AF.Exp, scale=-0.5), [ss], [ss])
                    yield
                    yo = R["xo"]()
                    P.dve(lambda e, yo=yo, xo=xo, ss=ss: e.scalar_tensor_tensor(out=yo[:], in0=xo[:], scalar=ss[:], in1=bclnf[:], op0=ALU.mult, op1=ALU.mult),
                          [xo, ss, bclnf], [yo])
                    dst = O["y_p"].h[s.i, r0:r0 + 128, :] if s.prompt else O["y_s"].h[r0:r0 + 128, :]
                    P.dma("sp", dst, yo[:], [yo], [O["y_p"] if s.prompt else O["y_s"]])
            return (s.T // 128, load_tile, body, lambda: None)

        def phase_end():
            P.flush()
            ph.close()
            P.stack = st

        run_multi([l0_stage1(s) for s in seqs])
        phase_end()
        ph = ExitStack()
        P.stack = ph
        alloc_att()
        for s in seqs:
            attend(s, s.QA, s.KA, s.VA, s.ZT0, s.OG0, 8, lambda h: h, 96, 64, MLA_SCALE, lambda h: h * 64, lambda h: h * 64, False, False)
            attend(s, s.QB, s.KB, s.VB, s.ZT0, s.OG0, 8, lambda h: h // 4, 64, 64, SWA_SCALE, lambda h: 512 + h * 64, lambda h: 512 + h * 64, True, True)
        phase_end()
        ph = ExitStack()
        P.stack = ph
        alloc_out()
        P.dma("pool", R["w_out"][:], I["w_out0"].h.rearrange("(c p) n -> p c n", p=128), [], [R["w_out"]])
        run_multi([outproj(s, 0, s.OG0, R["w_out"], s.x0, None, s.X1, False) for s in seqs])
        phase_end()

        ph = ExitStack()
        P.stack = ph
        alloc_stage()
        w_in1 = P.sb([128, 8, L1C], BF16, "w_in1")
        for c in range(8):
            P.dma("pool", w_in1[:, c, :], I["w_in1"].h[c * 128:(c + 1) * 128, :], [], [w_in1])
        w_qkv = w_in1
        zero_t = P.sb([128, 1536], F32, "zero")
        P.dve(lambda e: e.memset(zero_t[:], 0.0), [], [zero_t])
        nega = P.sb([128, 8], F32, "nega")
        P.act(lambda e: e.activation(out=nega[:], in_=bc1[:, 0:8], func=AF.Exp), [bc1], [nega])
        P.dve(lambda e: e.tensor_scalar(out=nega[:], in0=nega[:], scalar1=-1.0, scalar2=None, op0=ALU.mult), [nega], [nega])

        def kside1(s, col, kd_ap, kd_tb, vd_ap, vd_tb):
            tr_store([(kd_ap[:, k * 128:(k + 1) * 128], kd_tb) for k in range(2)], s.KD.h[:, col:col + 128].rearrange("(k p) t -> p k t", p=128), s.KD, 128)
            P.dma("sp", s.VD.h[col:col + 128, :], vd_ap, [vd_tb], [s.VD])

        def l1_stage1(s):
            T = s.T
            gb = gb_all[s.i]
            P.dma("sp", s.QKV.h[0:1, :], zero_t[0:1, :], [zero_t], [s.QKV])
            P.dma("sp", s.QKV.h[T + 1:T + 2, :], zero_t[0:1, :], [zero_t], [s.QKV])
            def load_tile(i):
                r0 = i * 128
                xt = x_ring()
                P.dma("sp", xt[:], s.X1.h[r0:r0 + 128, :], [s.X1], [xt])
                rta = None
                if not s.prompt:
                    rta = rt_ring()
                    P.dma("sp", rta[:], I["rt_att"].h[r0:r0 + 128, :, :], [], [rta])
                return xt, rta
            def body(i, loaded):
                r0 = i * 128
                xt, rta = loaded
                hT = yield from build_hT_g(xt, 1, s.v)
                yield
                for g3 in range(3):
                    ps = proj(hT, w_qkv, g3 * 512, 512)
                    yield
                    t = tm_ring()
                    P.copy(t[:, :], ps[:, :], [ps], [t])
                    P.dma("sp", s.QKV.h[1 + r0:1 + r0 + 128, g3 * 512:(g3 + 1) * 512], t[:, :], [t], [s.QKV])
                yield
                ps = proj(hT, w_in1, 1536, 16)
                yield
                sp_ = st8_ring()
                P.dve(lambda e, sp_=sp_, ps=ps: e.tensor_tensor(out=sp_[:, 0:8], in0=ps[:, 0:8], in1=bc1[:, 8:16], op=ALU.add), [ps, bc1], [sp_])
                P.act(lambda e, sp_=sp_: e.activation(out=sp_[:, 0:8], in_=sp_[:, 0:8], func=AF.Exp), [sp_], [sp_])
                P.act(lambda e, sp_=sp_: e.activation(out=sp_[:, 0:8], in_=sp_[:, 0:8], func=AF.Ln, bias=1.0), [sp_], [sp_])
                P.dve(lambda e, sp_=sp_, gb=gb, i=i: e.tensor_tensor(out=gb[:, i, 0:8], in0=sp_[:, 0:8], in1=nega[:], op=ALU.mult), [sp_, nega], [gb])
                P.act(lambda e, ps=ps, gb=gb, i=i: e.activation(out=gb[:, i, 8:16], in_=ps[:, 8:16], func=AF.Exp, scale=-1.0), [ps], [gb])
                P.dve(lambda e, gb=gb, i=i: e.tensor_scalar(out=gb[:, i, 8:16], in0=gb[:, i, 8:16], scalar1=1.0, scalar2=None, op0=ALU.add), [gb], [gb])
                P.dve(lambda e, gb=gb, i=i: e.reciprocal(out=gb[:, i, 8:16], in_=gb[:, i, 8:16]), [gb], [gb])
                yield
                ps = proj(hT, w_in1, 2064, 512)
                yield
                rs = rms_stat(ps[:, :], ps, 128, 4)
                yield
                t = tm_ring()
                tv = t[:, :].rearrange("p (h d) -> p h d", h=4)
                P.dve(lambda e, tv=tv, ps=ps, rs=rs: e.tensor_tensor(out=tv, in0=ps[:, :].rearrange("p (h d) -> p h d", h=4), in1=bc_last(rs, 0, 4, 128, 8), op=ALU.mult), [ps, rs], [t])
                P.pool(lambda e, tv=tv: e.tensor_tensor(out=tv, in0=tv, in1=bc_mid(bc1, 144, 4, 128, 400), op=ALU.mult), [t, bc1], [t])
                qd = tb_ring()
                if s.prompt:
                    P.copy(qd[:, 0:512], t[:, :], [t], [qd])
                else:
                    rope(tv, t, qd[:, 0:512].rearrange("p (h d) -> p h d", h=4), qd, rta, 4, 128)
                yield
                yield from tr_store_g([(qd[:, k * 128:(k + 1) * 128], qd) for k in range(4)], s.QD.h[:, r0:r0 + 128].rearrange("(k p) t -> p k t", p=128), s.QD, 128)
                yield
                ps = proj(hT, w_in1, 2576, 512)
                yield
                rs = rms_stat(ps[:, 0:256], ps, 128, 2)
                yield
                t = tm_ring()
                tv = t[:, 0:256].rearrange("p (h d) -> p h d", h=2)
                P.dve(lambda e, tv=tv, ps=ps, rs=rs: e.tensor_tensor(out=tv, in0=ps[:, 0:256].rearrange("p (h d) -> p h d", h=2), in1=bc_last(rs, 0, 2, 128, 8), op=ALU.mult), [ps, rs], [t])
                P.pool(lambda e, tv=tv: e.tensor_tensor(out=tv, in0=tv, in1=bc_mid(bc1, 272, 2, 128, 400), op=ALU.mult), [t, bc1], [t])
                P.act(lambda e, t=t, ps=ps: e.copy(out=t[:, 256:512], in_=ps[:, 256:512]), [ps], [t])
                kd = tb_ring()
                if s.prompt:
                    P.dma("sp", O["o_ak"].h[s.i, r0:r0 + 128, :], t[:, 0:256], [t], [O["o_ak"]])
                    P.dma("sp", O["o_av"].h[s.i, r0:r0 + 128, :], t[:, 256:512], [t], [O["o_av"]])
                    P.copy(kd[:, 0:512], t[:, :], [t], [kd])
                else:
                    rope(tv, t, kd[:, 0:256].rearrange("p (h d) -> p h d", h=2), kd, rta, 2, 128)
                    P.dve(lambda e, kd=kd, t=t: e.tensor_copy(out=kd[:, 256:512], in_=t[:, 256:512]), [t], [kd])
                kside1(s, r0, kd[:, 0:256], kd, kd[:, 256:512], kd)
                yield
                ps = proj(hT, w_in1, 1552, 512)
                yield
                t = tm_ring()
                P.act(lambda e, t=t, ps=ps: e.activation(out=t[:, :], in_=ps[:, :], func=AF.Silu), [ps], [t])
                P.dma("sp", s.ZC.h[r0:r0 + 128, :], t[:, :], [t], [s.ZC])
                yield
                ps = proj(hT, w_in1, 3088, 512)
                yield
                zs = tb_ring()
                P.act(lambda e, ps=ps, zs=zs: e.activation(out=zs[:, 0:512], in_=ps[:, :], func=AF.Silu), [ps], [zs])
                yield
                yield from tr_store_g([(zs[:, k * 128:(k + 1) * 128], zs) for k in range(4)], s.ZT1.h[512:1024, r0:r0 + 128].rearrange("(k p) t -> p k t", p=128), s.ZT1, 128)
            def post():
              if not s.prompt:
                for j in range(4):
                    c0 = j * 128
                    kd = tb_ring()
                    P.dma("pool", kd[:, 0:256], I["c_ak"].h[c0:c0 + 128, :], [], [kd])
                    P.dma("pool", kd[:, 256:512], I["c_av"].h[c0:c0 + 128, :], [], [kd])
                    kside1(s, s.T + c0, kd[:, 0:256], kd, kd[:, 256:512], kd)
            return (T // 128, load_tile, body, post)

        run_multi([l1_stage1(s) for s in seqs])
        phase_end()
        ph = ExitStack()
        P.stack = ph
        alloc_stage(False, need_x=False)
        gm = P.sb([128, 8, 128], F32, "gmask")
        P.dma("sp", gm[:], I["gmask"][:], [], [gm])
        amask = P.sb([128, 4, 128], F32, "amask")
        for d_ in range(2):
            P.dve(lambda e, d_=d_: e.tensor_scalar(out=amask[:, d_, :], in0=gm[:, d_, :], scalar1=-1.0, scalar2=1.0e4, op0=ALU.add, op1=ALU.mult), [gm], [amask])
            P.dve(lambda e, d_=d_: e.tensor_scalar(out=amask[:, 2 + d_, :], in0=gm[:, 2 + (1 - d_), :], scalar1=-1.0, scalar2=-1.0e4, op0=ALU.add, op1=ALU.mult), [gm], [amask])
        cw = P.sb([128, 3, 1536], F32, "convw")
        P.dma("sp", cw[:], I["bcconv"].h.rearrange("p (k c) -> p k c", k=3), [], [cw])
        SstAll = [[P.sb([128, 128], F32, "Sst") for _ in range(8)] for _ in range(3)]
        xin_ring = P.ring([128, 3, 1536], F32, "xin", 1)
        y_ring = P.ring([128, 1536], F32, "ycv", 3)
        yt_ring = P.ring([128, 1536], F32, "ytmp", 1)
        chain_rings = [P.ring([128, 128], F32, "gfc", 8) for _ in range(5)]
        ded = [[P.sb([128, 128], BF16 if j in (1, 2, 3, 5) else F32, "gded") for j in range(8)] for _ in range(5)]
        SbfAll = [[P.sb([128, 128], BF16, "Sbf") for _ in range(8)] for _ in range(3)]
        vn_rings = [P.ring([128, 128], BF16, "vnb", 2) for _ in range(5)]
        kq_ring = P.ring([128, 8, 128], F32, "kqT", 3)
        o_ring = P.ring([128, 512], F32, "otile", 4)
        of_ring = P.ring([128, 512], F32, "ofl", 2)
        s4_ring = P.ring([128, 16], F32, "s4", 12)

        def gdn_job(s, d):
            Sst = SstAll[s.i]
            T = s.T
            NT = T // 128
            gb = gb_all[s.i]
            U, UT, UTs = gm[:, d, :], gm[:, 1 - d, :], gm[:, 2 + (1 - d), :]
            def load(k):
                r0 = order[k] * 128
                xin = xin_ring()
                for kk in range(3):
                    P.dma("sp", xin[:, kk, :], s.QKV.h[r0 + kk:r0 + kk + 128, :], [s.QKV], [xin])
                return xin

            def prepgen(k, xin, next_load):
                i = order[k]
                r0 = i * 128
                y, yt = y_ring(), yt_ring()
                P.dve(lambda e, y=y, xin=xin: e.tensor_tensor(out=y[:], in0=xin[:, 0, :], in1=cw[:, 0, :], op=ALU.mult), [xin, cw], [y])
                P.pool(lambda e, yt=yt, xin=xin: e.tensor_tensor(out=yt[:], in0=xin[:, 1, :], in1=cw[:, 1, :], op=ALU.mult), [xin, cw], [yt])
                yield 10
                P.dve(lambda e, y=y, yt=yt: e.tensor_tensor(out=y[:], in0=y[:], in1=yt[:], op=ALU.add), [y, yt], [y])
                yield 3
                P.pool(lambda e, yt=yt, xin=xin: e.tensor_tensor(out=yt[:], in0=xin[:, 2, :], in1=cw[:, 2, :], op=ALU.mult), [xin, cw], [yt])
                next_load()
                yield 10
                P.dve(lambda e, y=y, yt=yt: e.tensor_tensor(out=y[:], in0=y[:], in1=yt[:], op=ALU.add), [y, yt], [y])
                yield 3
                P.act(lambda e, y=y: e.activation(out=y[:], in_=y[:], func=AF.Silu), [y], [y])
                yield 3
                P.pool(lambda e, yt=yt, y=y: e.tensor_tensor(out=yt[:, 0:1024], in0=y[:, 0:1024], in1=y[:, 0:1024], op=ALU.mult), [y], [yt])
                yield 7
                ss = st8_ring()
                P.dve(lambda e, ss=ss, yt=yt: e.tensor_reduce(out=ss[:, 0:8], in_=yt[:, 0:1024].rearrange("p (h d) -> p h d", h=8), axis=AX.X, op=ALU.add), [yt], [ss])
                yield 2
                P.act(lambda e, ss=ss: e.activation(out=ss[:, 0:8], in_=ss[:, 0:8], func=AF.Ln, bias=EPS), [ss], [ss])
                yield 0
                P.act(lambda e, ss=ss: e.activation(out=ss[:, 0:8], in_=ss[:, 0:8], func=AF.Exp, scale=-0.5), [ss], [ss])
                yield 0
                yv = y[:, 0:1024].rearrange("p (h d) -> p h d", h=8)
                P.dve(lambda e, yv=yv, ss=ss: e.tensor_tensor(out=yv, in0=yv, in1=bc_last(ss, 0, 8, 128, 8), op=ALU.mult), [y, ss], [y])
                yield 2
                kq = kq_ring()
                for half in range(2):
                    pt = P.ps()
                    for h in range(4):
                        P.tr(pt, pt[:, h * 128:(h + 1) * 128], y[:, (half * 4 + h) * 128:(half * 4 + h + 1) * 128], idf, [y])
                    yield 1
                    P.copy(kq[:, :, :].rearrange("p (h two) t -> p h two t", two=2)[:, :, 1 - half, :], pt[:, :].rearrange("p (h t) -> p h t", h=4), [pt], [kq])
                    yield 0
                g4 = gb[:, i, d * 4:(d + 1) * 4]
                be4 = gb[:, i, 8 + d * 4:8 + (d + 1) * 4]
                psc = P.ps()
                P.mm(psc, psc[:, 0:4], U, g4, True, True, [gm, gb])
                P.mm(psc, psc[:, 4:8], gm[:, 4, :], g4, True, True, [gm, gb])
                P.mm(psc, psc[:, 8:12], gm[:, 5, :], g4, True, True, [gm, gb])
                P.mm(psc, psc[:, 12:16], gm[:, 6, :], g4, True, True, [gm, gb])
                yield 1
                sc = s4_ring()
                ex = s4_ring()
                P.dve(lambda e, sc=sc, psc=psc: e.tensor_copy(out=sc[:, 0:4], in_=psc[:, 0:4]), [psc], [sc])
                P.act(lambda e, sc=sc, psc=psc: e.activation(out=sc[:, 8:16], in_=psc[:, 8:16], func=AF.Exp), [psc], [sc])
                yield 0
                P.dve(lambda e, sc=sc, psc=psc: e.tensor_tensor(out=sc[:, 4:8], in0=psc[:, 4:8], in1=sc[:, 0:4], op=ALU.subtract), [psc, sc], [sc])
                P.act(lambda e, sc=sc, ex=ex: e.activation(out=ex[:, 0:4], in_=sc[:, 0:4], func=AF.Exp), [sc], [ex])
                yield 0
                P.act(lambda e, sc=sc: e.activation(out=sc[:, 4:8], in_=sc[:, 4:8], func=AF.Exp), [sc], [sc])
                P.dve(lambda e, ex=ex, be4=be4: e.tensor_tensor(out=ex[:, 4:8], in0=ex[:, 0:4], in1=be4, op=ALU.mult), [ex, gb], [ex])
                Xs[k] = dict(i=i, r0=r0, y=y, kq=kq, g4=g4, be4=be4, sc=sc, ex=ex)

            def chain(h, X, ot, sl):
                i, y, kq, g4, be4, sc, ex = X['i'], X['y'], X['kq'], X['g4'], X['be4'], X['sc'], X['ex']
                fr = chain_rings[sl]
                bank = [0]

                def cps():
                    bank[0] += 1
                    return P.psum_banks[3 + sl]
                cb = d * 4 + h
                kT, qT = kq[:, 2 * h, :], kq[:, 2 * h + 1, :]
                k_tok, v_tok = y[:, (4 + h) * 128:(5 + h) * 128], y[:, (8 + h) * 128:(9 + h) * 128]
                psA = cps()
                P.mm(psA, psA[:, 0:256], kT, kq[:, 2 * h:2 * h + 2, :], True, True, [kq])
                grep = fr()
                P.dve(lambda e, grep=grep, h=h: e.tensor_scalar(out=grep[:], in0=ones_f[:], scalar1=g4[:, h:h + 1], scalar2=None, op0=ALU.mult), [ones_f, gb], [grep])
                yield
                P.mm(psA, psA[:, 256:384], grep[:], U, True, True, [grep, gm])
                yield
                EB = ded[sl][0]
                P.act(lambda e, EB=EB, psA=psA: e.activation(out=EB[:], in_=psA[:, 256:384], func=AF.Exp), [psA], [EB])
                DT, Dcs = fr(), fr()
                P.dve(lambda e, DT=DT, psA=psA, sc=sc, h=h: e.scalar_tensor_tensor(out=DT[:], in0=psA[:, 256:384], scalar=sc[:, h:h + 1], in1=amask[:, d, :],
                                                                                 op0=ALU.subtract, op1=ALU.add), [psA, sc, amask], [DT])
                P.dve(lambda e, Dcs=Dcs, psA=psA, sc=sc, h=h: e.scalar_tensor_tensor(out=Dcs[:], in0=psA[:, 256:384], scalar=sc[:, h:h + 1], in1=amask[:, 2 + d, :],
                                                                                   op0=ALU.subtract, op1=ALU.add), [psA, sc, amask], [Dcs])
                yield
                P.act(lambda e, DT=DT: e.activation(out=DT[:], in_=DT[:], func=AF.Exp), [DT], [DT])
                P.act(lambda e, Dcs=Dcs: e.activation(out=Dcs[:], in_=Dcs[:], func=AF.Exp, scale=-1.0), [Dcs], [Dcs])
                kdc, qg, vb_, kbg = ded[sl][2], ded[sl][3], ded[sl][6], ded[sl][7]
                P.dve(lambda e, vb_=vb_, h=h: e.tensor_scalar(out=vb_[:], in0=v_tok, scalar1=be4[:, h:h + 1], scalar2=None, op0=ALU.mult), [y, gb], [vb_])
                P.dve(lambda e, kbg=kbg, ex=ex, h=h: e.tensor_scalar(out=kbg[:], in0=k_tok, scalar1=ex[:, 4 + h:5 + h], scalar2=None, op0=ALU.mult), [y, ex], [kbg])
                yield
                Pk, AT = fr(), ded[sl][1]
                P.dve(lambda e, Pk=Pk, psA=psA, Dcs=Dcs, h=h: e.scalar_tensor_tensor(out=Pk[:], in0=psA[:, 0:128], scalar=be4[:, h:h + 1], in1=Dcs[:], op0=ALU.mult, op1=ALU.mult),
                      [psA, gb, Dcs], [Pk])
                P.dve(lambda e, AT=AT, psA=psA, DT=DT: e.scalar_tensor_tensor(out=AT[:], in0=psA[:, 128:256], scalar=GDN_SCALE, in1=DT[:], op0=ALU.mult, op1=ALU.mult),
                      [psA, DT], [AT])
                yield
                psB = cps()
                P.tr(psB, psB[:, 0:128], Pk[:], idf, [Pk])
                P.dve(lambda e, kdc=kdc, sc=sc, h=h: e.tensor_scalar(out=kdc[:], in0=k_tok, scalar1=sc[:, 4 + h:5 + h], scalar2=None, op0=ALU.mult), [y, sc], [kdc])
                P.dve(lambda e, qg=qg, EB=EB: e.scalar_tensor_tensor(out=qg[:], in0=qT, scalar=GDN_SCALE, in1=EB[:], op0=ALU.mult, op1=ALU.mult), [kq, EB], [qg])
                yield
                Qk, Tt = fr(), fr()
                P.act(lambda e, Qk=Qk, psB=psB: e.copy(out=Qk[:], in_=psB[:, 0:128]), [psB], [Qk])
                yield
                P.dve(lambda e, Tt=Tt, Qk=Qk: e.tensor_tensor(out=Tt[:], in0=gm[:, 7, :], in1=Qk[:], op=ALU.subtract), [gm, Qk], [Tt])
                for lev in range(1, 6):
                    psC = cps()
                    P.mm(psC, psC[:, 0:128], Qk[:], Pk[:], True, True, [Qk, Pk])
                    yield
                    Pn = fr()
                    P.act(lambda e, Pn=Pn, psC=psC: e.copy(out=Pn[:], in_=psC[:, 0:128]), [psC], [Pn])
                    yield
                    if lev < 5:
                        P.tr(psC, psC[:, 128:256], Pn[:], idf, [Pn])
                        Qn = fr()
                    psD = cps()
                    P.mm(psD, psD[:, 0:128], Pn[:], Tt[:], True, True, [Pn, Tt])
                    yield
                    if lev < 5:
                        P.act(lambda e, Qn=Qn, psC=psC: e.copy(out=Qn[:], in_=psC[:, 128:256]), [psC], [Qn])
                    Tn = fr()
                    P.dve(lambda e, Tn=Tn, psD=psD, Tt=Tt: e.tensor_tensor(out=Tn[:], in0=psD[:, 0:128], in1=Tt[:], op=ALU.add), [psD, Tt], [Tn])
                    yield
                    Pk, Tt = Pn, Tn
                    if lev < 5:
                        Qk = Qn
                psE = cps()
                P.mm(psE, psE[:, 0:128], Tt[:], vb_[:], True, True, [Tt, vb_])
                P.mm(psE, psE[:, 128:256], kbg[:], Tt[:], True, True, [kbg, Tt])
                yield
                u, wT = ded[sl][4], ded[sl][5]
                P.act(lambda e, u=u, psE=psE: e.copy(out=u[:], in_=psE[:, 0:128]), [psE], [u])
                P.dve(lambda e, wT=wT, psE=psE: e.tensor_copy(out=wT[:], in_=psE[:, 128:256]), [psE], [wT])
                yield
                Stb = Sst[cb]
                Sc = Stb[:, :]
                Sb = SbfAll[s.i][cb]
                for c in ((0, 1) if d == 0 else (1, 0)):
                    rc = slice(c * 64, (c + 1) * 64)
                    ps1 = cps()
                    P.mm(ps1, ps1[:, 0:128], wT[:], Sb[:], True, True, [wT, Sb])
                    P.mm(ps1, ps1[:, 128:256], qg[:], Sb[:], True, False, [qg, Sb])
                    yield
                    vn = vn_rings[sl]()
                    P.dve(lambda e, vn=vn, u=u, ps1=ps1: e.tensor_tensor(out=vn[:], in0=u[:], in1=ps1[:, 0:128], op=ALU.subtract), [u, ps1], [vn])
                    yield
                    P.mm(ps1, ps1[:, 128:256], AT[:], vn[:], False, True, [AT, vn])
                    P.mm(ps1, ps1[:, 256:384], kdc[rc, :], vn[rc, :], True, True, [kdc, vn])
                    yield
                    P.act(lambda e, ot=ot, ps1=ps1, rc=rc, h=h: e.copy(out=ot[rc, h * 128:(h + 1) * 128], in_=ps1[rc, 128:256]), [ps1], [ot])
                    P.dve(lambda e, ps1=ps1, sc=sc, c=c, h=h, Sc=Sc: e.scalar_tensor_tensor(out=Sc, in0=Sc, scalar=sc[:, 8 + c * 4 + h:9 + c * 4 + h], in1=ps1[:, 256:384],
                                                                                            op0=ALU.mult, op1=ALU.add), [Stb, sc, ps1], [Stb])
                    yield
                    P.pool(lambda e, Sb=Sb, Stb=Stb: e.tensor_copy(out=Sb[:], in_=Stb[:]), [Stb], [Sb])
                    yield

            order = list(range(NT)) if d == 0 else list(range(NT - 1, -1, -1))
            Xs, ots, remaining = {}, {}, {}

            def finish_tile(k):
                i = order[k]
                r0 = i * 128
                ot = ots[k]
                if d == 0:
                    P.dma("sp", s.OF.h[r0:r0 + 128, :], ot[:], [ot], [s.OF])
                else:
                    of_, zc = of_ring(), tm_ring()
                    P.dma("sp", of_[:], s.OF.h[r0:r0 + 128, :], [s.OF], [of_])
                    P.dma("sp", zc[:], s.ZC.h[r0:r0 + 128, :], [s.ZC], [zc])
                    P.dve(lambda e, ot=ot, of_=of_: e.tensor_tensor(out=ot[:], in0=ot[:], in1=of_[:], op=ALU.add), [ot, of_], [ot])
                    rs = rms_stat(ot[:], ot, 128, 4)
                    ov = ot[:].rearrange("p (h d) -> p h d", h=4)
                    P.dve(lambda e, ov=ov, rs=rs: e.tensor_tensor(out=ov, in0=ov, in1=bc_last(rs, 0, 4, 128, 8), op=ALU.mult), [ot, rs], [ot])
                    P.pool(lambda e, ov=ov: e.tensor_tensor(out=ov, in0=ov, in1=bc_mid(bc1, 16, 4, 128, 400), op=ALU.mult), [ot, bc1], [ot])
                    ogb = tb_ring()
                    P.dve(lambda e, ogb=ogb, ot=ot, zc=zc: e.tensor_tensor(out=ogb[:, 0:512], in0=ot[:], in1=zc[:], op=ALU.mult), [ot, zc], [ogb])
                    tr_store([(ogb[:, k * 128:(k + 1) * 128], ogb) for k in range(4)], s.OG1.h[0:512, r0:r0 + 128].rearrange("(k p) t -> p k t", p=128), s.OG1, 128)


            return dict(NT=NT, load=load, prepgen=prepgen, chain=chain, fin=finish_tile, Xs=Xs, ots=ots, rem=remaining)

        for s in seqs:
            for cb in range(8):
                if s.prompt:
                    P.dve(lambda e, cb=cb, s=s: e.memset(SstAll[s.i][cb][:], 0.0), [], [SstAll[s.i][cb]])
                else:
                    P.dma("sp", SstAll[s.i][cb][:], I["c_gdn"].h[cb // 4, cb % 4], [], [SstAll[s.i][cb]])
                P.pool(lambda e, cb=cb, s=s: e.tensor_copy(out=SbfAll[s.i][cb][:], in_=SstAll[s.i][cb][:]), [SstAll[s.i][cb]], [SbfAll[s.i][cb]])
        jobs = [gdn_job(s, 0) for s in seqs] + [gdn_job(s, 1) for s in seqs]
        tiles = [(ji, k) for ji, J in enumerate(jobs) for k in range(J["NT"])]
        queue = [(ti, h) for ti in range(len(tiles)) for h in range(4)]
        loaded, done_prep = {}, set()

        def start_load(ti):
            if ti < len(tiles) and ti not in loaded:
                loaded[ti] = jobs[tiles[ti][0]]["load"](tiles[ti][1])
        start_load(0)
        active, free_slots = [], [0, 1, 2, 3, 4]
        qi, rnd, GAP = 0, 0, 1
        P.ps_n = 3
        prep_i, prep_gen, prep_wait, cur_prep, finished_tiles = 0, None, 0, None, 0
        while qi < len(queue) or active or prep_gen is not None:
            if prep_gen is None and prep_i < len(tiles) and prep_i - finished_tiles <= 2:
                cur_prep = prep_i
                ji, k = tiles[prep_i]
                start_load(prep_i)
                prep_gen = jobs[ji]["prepgen"](k, loaded.pop(prep_i), lambda nxt=prep_i + 1: start_load(nxt))
                prep_i += 1
                prep_wait = 0
            if prep_gen is not None:
                if prep_wait > 0:
                    prep_wait -= 1
                else:
                    try:
                        prep_wait = next(prep_gen) or 0
                    except StopIteration:
                        done_prep.add(cur_prep)
                        prep_gen = None
            if qi < len(queue) and free_slots and rnd % GAP == 0 and queue[qi][0] in done_prep:
                ti, h = queue[qi]
                qi += 1
                ji, k = tiles[ti]
                J = jobs[ji]
                if k not in J["ots"]:
                    J["ots"][k] = o_ring()
                    J["rem"][k] = 4
                sl = free_slots.pop(0)
                active.append((ji, k, h, sl, J["chain"](h, J["Xs"][k], J["ots"][k], sl)))
            for item in list(active):
                ji, k, h, sl, g_ = item
                try:
                    next(g_)
                except StopIteration:
                    active.remove(item)
                    free_slots.append(sl)
                    J = jobs[ji]
                    J["rem"][k] -= 1
                    if J["rem"][k] == 0:
                        J["fin"](k)
                        finished_tiles += 1
            rnd += 1
        P.ps_n = 4
        for s in seqs:
            if s.prompt:
                for cb in range(8):
                    P.dma("sp", O["o_gdn"].h[s.i, cb // 4, cb % 4], SstAll[s.i][cb][:], [SstAll[s.i][cb]], [O["o_gdn"]])
        phase_end()
        ph = ExitStack()
        P.stack = ph
        alloc_att()
        for s in seqs:
            attend(s, s.QD, s.KD, s.VD, s.ZT1, s.OG1, 4, lambda h: h // 2, 128, 128, ATT_SCALE, lambda h: 512 + h * 128, lambda h: 512 + h * 128, False, False)
        phase_end()
        ph = ExitStack()
        P.stack = ph
        alloc_out()
        P.dma("pool", R["w_out"][:], I["w_out1"].h.rearrange("(c p) n -> p c n", p=128), [], [R["w_out"]])
        run_multi([outproj(s, 1, s.OG1, R["w_out"], lambda r0, s=s: s.X1.h[r0:r0 + 128, :], s.X1, None, True) for s in seqs])
        phase_end()
        P.finish()
    return nc


def _rope_host(n_tok, rot):
    q = rot // 4
    inv = (10000.0 ** (-np.arange(q, dtype=np.float32) / q)).astype(np.float32)
    t = np.arange(n_tok)
    pos = np.stack([t // 64, t % 64], -1).astype(np.float32)
    ang = pos[:, :, None] * inv
    c, s_ = np.cos(ang).astype(np.float32), np.sin(ang).astype(np.float32)
    Cx = np.stack([c, c], 2).reshape(n_tok, rot)
    Sx = np.stack([-s_, s_], 2).reshape(n_tok, rot)
    return np.ascontiguousarray(np.stack([Cx, Sx], 1)).astype(np.float32)


_NC = None


def kernel(**inp):
    global _NC
    f = lambda a: np.ascontiguousarray(np.asarray(a, dtype=np.float32))
    x_prompt, x_sample = f(inp["x_prompt"]), f(inp["x_sample"])
    fm = lambda v: np.ascontiguousarray(v.reshape(-1, 128).T)
    rep = lambda v: np.ascontiguousarray(np.broadcast_to(v.reshape(1, -1), (128, v.size)))
    b0, b1 = f(inp["b_mod0"]), f(inp["b_mod1"])
    shared = {
        "w_mod0": f(inp["w_mod0"]), "w_mod1": f(inp["w_mod1"]),
        "bm_fm": np.ascontiguousarray(np.stack([fm(b0[:2048]), fm(b1[:2048])], 1)),
        "bm_row": np.ascontiguousarray(np.broadcast_to(np.stack([b0[2048:], b1[2048:]], 0)[None], (2, 2, D))),
        "ln_fm": np.ascontiguousarray(np.stack([fm(f(inp["ln0"])), fm(f(inp["ln1"]))], 1)),
        "w_in0": f(inp["w_in0"]), "w_in1": f(inp["w_in1"]), "w_uq": f(inp["w_uq"]),
        "w_out0": f(inp["w_out0"]), "w_out1": f(inp["w_out1"]),
        "bc0": rep(np.concatenate([f(inp["mla_q_norm"]), f(inp["mla_kv_norm"]), f(inp["swa_sink"])])),
        "bc1": rep(np.concatenate([f(inp["gdn_a_log"]).reshape(-1), f(inp["gdn_dt_bias"]).reshape(-1), f(inp["gdn_norm"]),
                                   f(inp["att_q_norm"]), f(inp["att_k_norm"])])),
        "bcconv": rep(f(inp["gdn_conv"]).reshape(-1)), "bclnf": rep(f(inp["ln_f"])),
        "rt_mla": _rope_host(2048, 32), "rt_swa": _rope_host(2048, 64), "rt_att": _rope_host(2048, 128),
        "ident": np.eye(128, dtype=np.float32),
    }
    wk = f(inp["w_ukv"]).reshape(256, 8, 128)
    shared["w_ukv"] = np.ascontiguousarray(np.concatenate([wk[:, :, :64].reshape(256, 512), wk[:, :, 64:].reshape(256, 512)], 1))
    sel = np.zeros((2, 2, 128), np.float32)
    sel[0, 0], sel[1, 1] = 1, 1
    shared["sel"] = sel
    r, c = np.arange(128)[:, None], np.arange(512)[None, :]
    shared["swamask"] = np.ascontiguousarray(np.stack([(np.abs(c - ((mk - 1) * 128 + r)) <= 128) for mk in range(6)], 1).astype(np.float32))
    m, i_ = np.arange(128)[:, None], np.arange(128)[None, :]
    same = (m // 64) == (i_ // 64)
    Uf, Ub = same & (m <= i_), same & (m >= i_)
    eye = np.eye(128, dtype=bool)
    gms = [Uf, Ub, Uf & ~eye, Ub & ~eye, same, np.broadcast_to(m < 64, (128, 128)), np.broadcast_to(m >= 64, (128, 128)), eye]
    shared["gmask"] = np.ascontiguousarray(np.stack([g.astype(np.float32) for g in gms], 1))
    c_all, c_ctx = f(inp["c"]), f(inp["c_ctx"])
    in_maps = []
    for b in range(8):
        mp = dict(shared)
        mp["xp"] = np.ascontiguousarray(x_prompt[2 * b:2 * b + 2])
        mp["xs"] = np.ascontiguousarray(x_sample[b])
        mp["c_ckv"] = f(inp["cache_l0_mla_ckv"][b])
        mp["c_kr"] = f(inp["cache_l0_mla_krope"][b])
        mp["c_sk"] = f(inp["cache_l0_swa_k"][b]).reshape(512, 128)
        mp["c_sv"] = f(inp["cache_l0_swa_v"][b]).reshape(512, 128)
        mp["c_gdn"] = f(inp["state_l1_gdn"][b])
        mp["c_ak"] = f(inp["cache_l1_attn_k"][b]).reshape(512, 256)
        mp["c_av"] = f(inp["cache_l1_attn_v"][b]).reshape(512, 256)
        mp["cT"] = np.ascontiguousarray(np.stack([fm(c_ctx), fm(c_all[b])], -1))
        in_maps.append(mp)
    if _NC is None:
        _NC = build_program()
    res = run_bass_kernel_spmd(_NC, in_maps, core_ids=list(range(8)))
    R = res.results
    if DEBUG:
        DBG_OUT["x1"] = np.asarray(R[0]["dbg_x1"])
        DBG_OUT["og0"] = np.asarray(R[0]["dbg_og0"])
        for k in ("qa", "ka", "va", "zt0"):
            DBG_OUT[k] = np.asarray(R[0]["dbg_" + k]).astype(np.float32)
    cat = lambda k: np.concatenate([np.asarray(r_[k], dtype=np.float32) for r_ in R], 0)
    y_p = cat("y_p")
    y_s = np.stack([np.asarray(r_["y_s"], dtype=np.float32) for r_ in R], 0)
    return (y_p, y_s, cat("o_ckv"), cat("o_kr"), cat("o_sk").reshape(16, 256, 2, 64), cat("o_sv").reshape(16, 256, 2, 64),
            cat("o_gdn"), cat("o_ak").reshape(16, 256, 2, 128), cat("o_av").reshape(16, 256, 2, 128))
```

```python
import math
from contextlib import ExitStack
import numpy as np
import concourse.bass as bass
import concourse.mybir as mybir
from concourse.bass_utils import run_bass_kernel_spmd

F32 = mybir.dt.float32
BF16 = mybir.dt.bfloat16
AF = mybir.ActivationFunctionType
ALU = mybir.AluOpType
AX = mybir.AxisListType

D = 1024
EPS = 1e-6
L0C, L1C = 2464, 3600
MLA_SCALE = 96 ** -0.5
SWA_SCALE = 64 ** -0.5
ATT_SCALE = 128 ** -0.5
GDN_SCALE = 128 ** -0.5
DEBUG = False
LOOK = 3
DBG_OUT = {}


class TB:
    __slots__ = ("name", "h", "last_w", "readers", "excl")

    def __init__(self, name, h, excl=False):
        self.name, self.h, self.last_w, self.readers, self.excl = name, h, None, {}, excl

    def __getitem__(self, k):
        return self.h[k]


class Op:
    __slots__ = ("eng", "fn", "reads", "writes", "deps", "sig", "ev", "dma")

    def __init__(self, eng, fn, reads, writes, dma):
        self.eng, self.fn, self.reads, self.writes, self.dma = eng, fn, reads, writes, dma
        self.deps, self.sig, self.ev = (), False, None


class _Recorder:
    def __init__(self):
        self.calls = []

    def __getattr__(self, name):
        def f(*args, **kwargs):
            self.calls.append((name, args, kwargs))
        return f


class Prog:
    NDMA = 12

    def __init__(self, nc, stack):
        self.nc, self.stack, self.ops, self.n_sb = nc, stack, [], 0
        self.psum_banks, self.ps_i, self.tog, self.acc_i = [], 0, 0, 0
        self.cur, self.ps_slotted, self.ps_si = 0, False, [0, 0]

    def sb(self, shape, dt=F32, name="t"):
        self.n_sb += 1
        h = self.stack.enter_context(self.nc.sbuf_tensor(f"{name}_{self.n_sb}", list(shape), dt))
        return TB(name, h)

    def ring(self, shape, dt, name, n=2, slots=1):
        bufs = [[self.sb(shape, dt, name) for _ in range(n)] for _ in range(slots)]
        st = [0] * slots

        def nxt():
            c = self.cur if slots > 1 else 0
            st[c] += 1
            return bufs[c][st[c] % n]
        return nxt

    def dram(self, name, shape, dt):
        return TB(name, self.nc.dram_tensor(name, list(shape), dt))

    def init_psum(self, n=8):
        for i in range(n):
            h = self.stack.enter_context(self.nc.psum_tensor(f"psb{i}", [128, 512], F32))
            self.psum_banks.append(TB(f"ps{i}", h, excl=True))

    def ps(self):
        if self.ps_slotted:
            self.ps_si[self.cur] += 1
            return self.psum_banks[4 * self.cur + self.ps_si[self.cur] % 4]
        t = self.psum_banks[self.ps_i % 4]
        self.ps_i += 1
        return t

    def ps_acc(self):
        t = self.psum_banks[4 + self.acc_i % 4]
        self.acc_i += 1
        return t

    def add(self, eng, fn, reads, writes, dma=False):
        rec = _Recorder()
        fn(rec)
        assert len(rec.calls) == 1, rec.calls
        name, args, kwargs = rec.calls[0]
        self.ops.append(Op(eng, lambda e: getattr(e, name)(*args, **kwargs), [r for r in reads if r is not None],
                           [w for w in writes if w is not None], dma))

    def pe(self, fn, r, w): self.add("pe", fn, r, w)
    def act(self, fn, r, w): self.add("act", fn, r, w)
    def dve(self, fn, r, w): self.add("dve", fn, r, w)
    def pool(self, fn, r, w): self.add("pool", fn, r, w)

    def any2(self, fn, r, w):
        self.tog += 1
        self.add("dve" if self.tog % 2 else "pool", fn, r, w)

    def dma(self, q, out, in_, reads, writes):
        self.add(q, lambda e: e.dma_start(out=out, in_=in_), reads, writes, dma=True)

    def mm(self, ps, out, lhsT, rhs, start, stop, reads):
        self.pe(lambda e: e.matmul(out, lhsT=lhsT, rhs=rhs, start=start, stop=stop), reads, [ps])

    def tr(self, ps, out, in_, ident, reads):
        self.pe(lambda e: e.transpose(out=out, in_=in_, identity=ident[:]), reads + [ident], [ps])

    def copy(self, out, in_, reads, writes):
        self.tog += 1
        if self.tog % 2:
            self.act(lambda e: e.copy(out=out, in_=in_), reads, writes)
        else:
            self.dve(lambda e: e.tensor_copy(out=out, in_=in_), reads, writes)

    def setup_sync(self):
        nc, st = self.nc, self.stack
        self.engs = {"pe": nc.tensor, "act": nc.scalar, "dve": nc.vector, "pool": nc.gpsimd, "sp": nc.sync}
        self.csem = {e: st.enter_context(nc.semaphore(f"c_{e}")) for e in ("pe", "act", "dve", "pool")}
        self.ccnt = {e: 0 for e in self.csem}
        self.dsem = {q: [st.enter_context(nc.semaphore(f"d_{q}{k}")) for k in range(self.NDMA)] for q in ("sp", "pool", "act")}
        self.dcnt = {q: [0] * self.NDMA for q in self.dsem}
        self.dnext = {q: 0 for q in self.dsem}
        self.waited = {}
        self.done = 0
        self.bar_ev = None

    def wait(self, eng, ev):
        s, v = ev
        key = (eng, id(s))
        if self.waited.get(key, 0) >= v:
            return
        self.waited[key] = v
        self.engs[eng].wait_ge(s, v)

    def flush(self):
        ops, start = self.ops, self.done
        for i in range(start, len(ops)):
            op = ops[i]
            deps = set()
            for t in op.reads:
                if t.last_w is not None:
                    deps.add(t.last_w)
                if t.excl:
                    deps.update(t.readers.values())
            for t in op.writes:
                if t.last_w is not None:
                    deps.add(t.last_w)
                deps.update(t.readers.values())
            deps.discard(i)
            deps = {d for d in deps if d >= start}
            if op.eng == "pe":
                deps = {d for d in deps if not (ops[d].eng == "pe" and not ops[d].dma)}
            op.deps = sorted(deps)
            for d in op.deps:
                ops[d].sig = True
            for t in op.writes:
                t.last_w, t.readers = i, {}
            for t in op.reads:
                if t.excl:
                    t.last_w, t.readers = i, {}
                else:
                    t.readers[("dma", i) if op.dma else op.eng] = i
        if self.bar_ev is not None:
            for e in ("pe", "act", "pool", "sp"):
                self.wait(e, self.bar_ev)
        for i in range(start, len(ops)):
            op = ops[i]
            e = op.eng
            for d in op.deps:
                self.wait(e, ops[d].ev)
            if op.dma:
                k = self.dnext[e] % self.NDMA
                self.dnext[e] += 1
                sm = self.dsem[e][k]
                if self.dcnt[e][k] > 0:
                    self.wait(e, (sm, self.dcnt[e][k]))
                ins = op.fn(self.engs[e])
                self.dcnt[e][k] += 16
                ins.then_inc(sm, 16)
                op.ev = (sm, self.dcnt[e][k])
            else:
                ins = op.fn(self.engs[e])
                if op.sig:
                    self.ccnt[e] += 1
                    ins.then_inc(self.csem[e], 1)
                    op.ev = (self.csem[e], self.ccnt[e])
        self.done = len(ops)
        for e in ("pe", "act", "pool"):
            self.ccnt[e] += 1
            self.engs[e].drain().then_inc(self.csem[e], 1)
            self.wait("dve", (self.csem[e], self.ccnt[e]))
        for q in self.dsem:
            for k in range(self.NDMA):
                if self.dcnt[q][k] > 0:
                    self.wait("dve", (self.dsem[q][k], self.dcnt[q][k]))
        self.ccnt["dve"] += 1
        self.engs["dve"].drain().then_inc(self.csem["dve"], 1)
        self.bar_ev = (self.csem["dve"], self.ccnt["dve"])

    def finish(self):
        for e in ("pe", "act", "pool", "sp"):
            self.wait(e, self.bar_ev)
        for q in ("sp", "pool"):
            for k in range(self.NDMA):
                if self.dcnt[q][k] > 0:
                    self.wait(q, (self.dsem[q][k], self.dcnt[q][k]))


def bc_last(tb, col0, n_outer, n_inner, pitch):
    return bass.AP(tb.h, col0, [[pitch, 128], [1, n_outer], [0, n_inner]])


def bc_mid(tb, col0, n_rep, n_inner, pitch):
    return bass.AP(tb.h, col0, [[pitch, 128], [0, n_rep], [1, n_inner]])


class Seq:
    pass


def build_program():
    nc = bass.Bass("TRN2", target_bir_lowering=False)

    def din(name, shape):
        return TB(name, nc.dram_tensor(name, list(shape), F32, kind="ExternalInput"))

    def dout(name, shape):
        return TB(name, nc.dram_tensor(name, list(shape), F32, kind="ExternalOutput"))

    I = {}
    for name, shape in [
        ("xp", (2, 256, D)), ("xs", (2048, D)), ("c_ckv", (512, 256)), ("c_kr", (512, 32)), ("c_sk", (512, 128)),
        ("c_sv", (512, 128)), ("c_gdn", (2, 4, 128, 128)), ("c_ak", (512, 256)), ("c_av", (512, 256)),
        ("cT", (128, 8, 2)), ("w_mod0", (D, 3 * D)), ("w_mod1", (D, 3 * D)), ("bm_fm", (128, 2, 16)),
        ("bm_row", (2, 2, D)), ("ln_fm", (128, 2, 8)), ("w_in0", (D, L0C)), ("w_in1", (D, L1C)), ("w_uq", (384, 768)),
        ("w_ukv", (256, 1024)), ("w_out0", (D, D)), ("w_out1", (D, D)), ("bc0", (128, 648)), ("bc1", (128, 400)),
        ("bcconv", (128, 4608)), ("bclnf", (128, D)), ("rt_mla", (2048, 2, 32)), ("rt_swa", (2048, 2, 64)),
        ("rt_att", (2048, 2, 128)), ("ident", (128, 128)), ("sel", (2, 2, 128)), ("swamask", (128, 6, 512)),
        ("gmask", (128, 8, 128)),
    ]:
        I[name] = din(name, shape)
    O = {}
    for name, shape in [
        ("y_p", (2, 256, D)), ("y_s", (2048, D)), ("o_ckv", (2, 256, 256)), ("o_kr", (2, 256, 32)),
        ("o_sk", (2, 256, 128)), ("o_sv", (2, 256, 128)), ("o_gdn", (2, 2, 4, 128, 128)), ("o_ak", (2, 256, 256)),
        ("o_av", (2, 256, 256)),
    ]:
        O[name] = dout(name, shape)

    with ExitStack() as st:
        P = Prog(nc, st)
        P.init_psum()
        P.setup_sync()
        R = {}

        idf = P.sb([128, 128], F32, "idf")
        idb = P.sb([128, 128], BF16, "idb")
        ones_bf = P.sb([128, 128], BF16, "ones")
        ones_f = P.sb([128, 128], F32, "onesf")
        P.dma("sp", idf[:], I["ident"][:], [], [idf])
        P.dma("pool", idb[:], I["ident"][:], [], [idb])
        P.dve(lambda e: e.memset(ones_bf[:], 1.0), [], [ones_bf])
        P.dve(lambda e: e.memset(ones_f[:], 1.0), [], [ones_f])
        sel = P.sb([2, 2, 128], F32, "sel")
        P.dma("sp", sel[:], I["sel"][:], [], [sel])
        bc0 = P.sb([128, 648], F32, "bc0")
        P.dma("sp", bc0[:], I["bc0"][:], [], [bc0])
        bc1 = P.sb([128, 400], F32, "bc1")
        P.dma("sp", bc1[:], I["bc1"][:], [], [bc1])
        esink = P.sb([128, 8], F32, "esink")
        P.act(lambda e: e.activation(out=esink[:], in_=bc0[:, 640:648], func=AF.Exp), [bc0], [esink])

        g_fm = [[P.sb([128, 8], F32, "gfm") for v in range(2)] for l in range(2)]
        s_fm = [[P.sb([128, 8], F32, "sfm") for v in range(2)] for l in range(2)]
        gate_bc = [[P.sb([128, D], F32, "gatebc") for v in range(2)] for l in range(2)]
        bclnf = P.sb([128, D], F32, "bclnf")
        P.dma("sp", bclnf[:], I["bclnf"][:], [], [bclnf])
        gb_all = [P.sb([128, T_ // 128, 16], F32, "gb") for T_ in (256, 256, 2048)]
        x_ring = xn_ring = hT_ring = rt_ring = rp_ring = None
        junk_ring = st_ring = st8_ring = stg_ring = tm_ring = tb_ring = None

        def alloc_stage(full=True, need_x=True):
            nonlocal x_ring, xn_ring, hT_ring, rt_ring, rp_ring, junk_ring, st_ring, st8_ring, stg_ring, tm_ring, tb_ring
            ns = 2 if full else 1
            if need_x:
                x_ring = P.ring([128, D], F32, "xt", 2, ns)
            junk_ring = P.ring([128, D], BF16, "junk", 1 if full else 2, ns)
            st_ring = P.ring([128, 1], F32, "stat", 8, ns)
            st8_ring = P.ring([128, 8], F32, "stat8", 8, ns)
            stg_ring = P.ring([128, 8, 128], BF16, "stg", 4, ns)
            tm_ring = P.ring([128, 512], F32, "tm32", 3 if full else 4, ns)
            tb_ring = P.ring([128, 768], BF16, "tmb", 4, ns)
            if full:
                xn_ring = P.ring([128, D], BF16, "xn", 1, ns)
                hT_ring = P.ring([128, 8, 128], BF16, "hT", 1, ns)
                rt_ring = P.ring([128, 2, 128], F32, "rt", 4, ns)
                rp_ring = P.ring([128, 512], F32, "rp32", 2, ns)

        def run_multi(jobs):
            G = [(j, i) for j, jb in enumerate(jobs) for i in range(jb[0])]
            run_slotted(len(G), lambda g: jobs[G[g][0]][1](G[g][1]), lambda g, loaded: jobs[G[g][0]][2](G[g][1], loaded))
            for jb in jobs:
                jb[3]()

        def run_slotted(NT, load_tile, body, stag=1):
            P.ps_slotted = True
            pre = {}

            def ld(slot, i):
                if i < NT and i not in pre:
                    P.cur = slot
                    pre[i] = load_tile(i)
            gens, nexti = [None, None], [0, 1]
            ld(0, 0)
            ld(1, 1)
            rnd = 0
            while True:
                busy = False
                for slot in (0, 1):
                    if gens[slot] is None and nexti[slot] < NT and not (slot == 1 and rnd < stag):
                        i = nexti[slot]
                        nexti[slot] += 2
                        ld(slot, i + 2)
                        P.cur = slot
                        gens[slot] = body(i, pre.pop(i))
                    if gens[slot] is not None:
                        busy = True
                        P.cur = slot
                        try:
                            next(gens[slot])
                        except StopIteration:
                            gens[slot] = None
                rnd += 1
                if not busy and all(n >= NT for n in nexti) and rnd > stag:
                    break
            P.cur, P.ps_slotted = 0, False

        ph = ExitStack()
        P.stack = ph
        cT = P.sb([128, 8, 2], F32, "cT")
        P.dma("sp", cT[:], I["cT"][:], [], [cT])
        scT = P.sb([128, 8, 2], BF16, "scT")
        P.act(lambda e: e.activation(out=scT[:], in_=cT[:], func=AF.Silu), [cT], [scT])
        bmfm = P.sb([128, 2, 16], F32, "bmfm")
        P.dma("sp", bmfm[:], I["bm_fm"][:], [], [bmfm])
        bmrow = P.sb([2, 2, D], F32, "bmrow")
        P.dma("sp", bmrow[:], I["bm_row"][:], [], [bmrow])
        lnfm = P.sb([128, 2, 8], F32, "lnfm")
        P.dma("sp", lnfm[:], I["ln_fm"][:], [], [lnfm])
        wm_ring = P.ring([128, 8, 512], BF16, "wm", 2)
        modfm = P.sb([128, 2, 16], F32, "modfm")
        grow = P.sb([2, D], F32, "grow")
        for l in range(2):
            wsrc = I["w_mod0" if l == 0 else "w_mod1"]
            psm = P.ps_acc()
            for blk in range(6):
                wm = wm_ring()
                P.dma("pool", wm[:], wsrc.h.rearrange("(c p) n -> p c n", p=128)[:, :, blk * 512:(blk + 1) * 512], [], [wm])
                if blk < 4:
                    for j in range(4):
                        o = (blk * 4 + j) * 2
                        for ch in range(8):
                            P.mm(psm, psm[:, o:o + 2], wm[:, ch, j * 128:(j + 1) * 128], scT[:, ch, :], ch == 0, ch == 7, [wm, scT])
                else:
                    psg = P.ps()
                    for ch in range(8):
                        P.mm(psg, psg[0:2, :], scT[:, ch, :], wm[:, ch, :], ch == 0, ch == 7, [wm, scT])
                    hf = blk - 4
                    P.dve(lambda e, psg=psg, hf=hf, l=l: e.tensor_tensor(out=grow[0:2, hf * 512:(hf + 1) * 512], in0=psg[0:2, :],
                                                                       in1=bmrow[0:2, l, hf * 512:(hf + 1) * 512], op=ALU.add), [psg, bmrow], [grow])
            for v in range(2):
                P.dve(lambda e, v=v, l=l, psm=psm: e.tensor_tensor(out=modfm[:, v, :], in0=psm[:, 0:32].rearrange("p (j v) -> p v j", v=2)[:, v, :], in1=bmfm[:, l, :], op=ALU.add), [psm, bmfm], [modfm])
                P.dve(lambda e, v=v, l=l: e.scalar_tensor_tensor(out=g_fm[l][v][:], in0=modfm[:, v, 8:16], scalar=1.0, in1=lnfm[:, l, :],
                                                                 op0=ALU.add, op1=ALU.mult), [modfm, lnfm], [g_fm[l][v]])
                P.dve(lambda e, v=v, l=l: e.tensor_copy(out=s_fm[l][v][:], in_=modfm[:, v, 0:8]), [modfm], [s_fm[l][v]])
                for hf in range(2):
                    psb = P.ps()
                    P.mm(psb, psb[:, :], sel[0:2, v, :], grow[0:2, hf * 512:(hf + 1) * 512], True, True, [sel, grow])
                    P.copy(gate_bc[l][v][:, hf * 512:(hf + 1) * 512], psb[:, :], [psb], [gate_bc[l][v]])

        P.flush()
        ph.close()
        P.stack = st
        seqs = []
        for si in range(3):
            s = Seq()
            s.i, s.prompt = si, si < 2
            s.T = 256 if s.prompt else 2048
            s.Tk = s.T + (0 if s.prompt else 512)
            s.v = 0 if s.prompt else 1
            s.x0 = (lambda r0, si=si: I["xp"].h[si, r0:r0 + 128, :]) if s.prompt else (lambda r0: I["xs"].h[r0:r0 + 128, :])
            s.x0tb = I["xp"] if s.prompt else I["xs"]
            n = f"s{si}"
            T, Tk = s.T, s.Tk
            if DEBUG and not s.prompt:
                s.X1 = TB("dbg_x1", nc.dram_tensor("dbg_x1", [T, D], F32, kind="ExternalOutput"))
            else:
                s.X1 = P.dram(n + "X1", [T, D], F32)
            if DEBUG and not s.prompt:
                s.QA = TB("dbg_qa", nc.dram_tensor("dbg_qa", [768, T], BF16, kind="ExternalOutput"))
                s.KA = TB("dbg_ka", nc.dram_tensor("dbg_ka", [768, Tk], BF16, kind="ExternalOutput"))
                s.VA = TB("dbg_va", nc.dram_tensor("dbg_va", [Tk, 512], BF16, kind="ExternalOutput"))
                s.ZT0 = TB("dbg_zt0", nc.dram_tensor("dbg_zt0", [D, T], BF16, kind="ExternalOutput"))
            else:
                s.QA = P.dram(n + "QA", [768, T], BF16)
                s.KA = P.dram(n + "KA", [768, Tk], BF16)
                s.VA = P.dram(n + "VA", [Tk, 512], BF16)
                s.ZT0 = P.dram(n + "ZT0", [D, T], BF16)
            s.QB = P.dram(n + "QB", [512, T], BF16)
            s.KB = P.dram(n + "KB", [128, Tk], BF16)
            s.VB = P.dram(n + "VB", [Tk, 128], BF16)
            if DEBUG and not s.prompt:
                s.OG0 = TB("dbg_og0", nc.dram_tensor("dbg_og0", [D, T], BF16, kind="ExternalOutput"))
            else:
                s.OG0 = P.dram(n + "OG0", [D, T], BF16)
            s.QD = P.dram(n + "QD", [512, T], BF16)
            s.KD = P.dram(n + "KD", [256, Tk], BF16)
            s.VD = P.dram(n + "VD", [Tk, 256], BF16)
            s.ZT1 = P.dram(n + "ZT1", [D, T], BF16)
            s.OG1 = P.dram(n + "OG1", [D, T], BF16)
            s.QKV = P.dram(n + "QKV", [T + 2, 1536], F32)
            s.ZC = P.dram(n + "ZC", [T, 512], F32)
            s.OF = P.dram(n + "OF", [T, 512], F32)
            seqs.append(s)

        def build_hT_g(xt, l, v):
            junk, ss, xn, hT = junk_ring(), st_ring(), xn_ring(), hT_ring()
            P.dve(lambda e: e.memset(ss[:], 0.0), [], [ss])
            P.act(lambda e: e.activation(out=junk[:], in_=xt[:], func=AF.Square, accum_out=ss[:]), [xt], [junk, ss])
            P.act(lambda e: e.activation(out=ss[:], in_=ss[:], func=AF.Ln, scale=1.0 / D, bias=EPS), [ss], [ss])
            P.act(lambda e: e.activation(out=ss[:], in_=ss[:], func=AF.Exp, scale=-0.5), [ss], [ss])
            P.act(lambda e: e.activation(out=xn[:], in_=xt[:], func=AF.Copy, scale=ss[:]), [xt, ss], [xn])
            yield
            pt = P.ps()
            ptb = pt.h[:].bitcast(BF16)
            for c in range(8):
                P.tr(pt, ptb[:, c * 128:(c + 1) * 128], xn[:, c * 128:(c + 1) * 128], idb, [xn])
            yield
            g, s_ = g_fm[l][v], s_fm[l][v]
            for c in range(8):
                if c % 2 == 0:
                    P.act(lambda e, c=c: e.activation(out=hT[:, c, :], in_=ptb[:, c * 128:(c + 1) * 128], func=AF.Identity,
                                                      scale=g[:, c:c + 1], bias=s_[:, c:c + 1]), [pt, g, s_], [hT])
                else:
                    P.dve(lambda e, c=c: e.tensor_scalar(out=hT[:, c, :], in0=ptb[:, c * 128:(c + 1) * 128], scalar1=g[:, c:c + 1],
                                                         scalar2=s_[:, c:c + 1], op0=ALU.mult, op1=ALU.add), [pt, g, s_], [hT])
            return hT

        def proj(hT, W, col0, ncols):
            ps = P.ps()
            for c in range(8):
                P.mm(ps, ps[:, 0:ncols], hT[:, c, :], W[:, c, col0:col0 + ncols], c == 0, c == 7, [hT, W])
            return ps

        def rms_stat(src_ap, src_tb, n, nh=1):
            junk = junk_ring()
            if nh == 1:
                ss = st_ring()
                P.dve(lambda e: e.memset(ss[:], 0.0), [], [ss])
                P.act(lambda e: e.activation(out=junk[:, 0:n], in_=src_ap, func=AF.Square, accum_out=ss[:]), [src_tb], [junk, ss])
                sv = ss[:]
            else:
                ss = st8_ring()
                P.act(lambda e: e.activation(out=junk[:, 0:nh * n], in_=src_ap, func=AF.Square), [src_tb], [junk])
                P.dve(lambda e: e.tensor_reduce(out=ss[:, 0:nh], in_=junk[:, 0:nh * n].rearrange("p (h d) -> p h d", h=nh), axis=AX.X, op=ALU.add), [junk], [ss])
                sv = ss[:, 0:nh]
            P.act(lambda e: e.activation(out=sv, in_=sv, func=AF.Ln, scale=1.0 / n, bias=EPS), [ss], [ss])
            P.act(lambda e: e.activation(out=sv, in_=sv, func=AF.Exp, scale=-0.5), [ss], [ss])
            return ss

        def rope(src_ap3, src_tb, out_ap3, out_tb, rt, H, R):
            q = R // 4
            t1, t2 = rp_ring(), rp_ring()
            t1v = t1[:, 0:H * R].rearrange("p (h r) -> p h r", h=H)
            t2v = t2[:, 0:H * R].rearrange("p (h r) -> p h r", h=H)
            P.dve(lambda e: e.tensor_tensor(out=t1v, in0=src_ap3, in1=bc_mid(rt, 0, H, R, 256), op=ALU.mult), [src_tb, rt], [t1])
            for hf in range(2):
                for j in range(2):
                    a = hf * 2 * q + j * q
                    b = hf * 2 * q + (1 - j) * q
                    P.dve(lambda e, a=a, b=b: e.tensor_tensor(out=t2v[:, :, a:a + q], in0=src_ap3[:, :, b:b + q],
                                                              in1=bc_mid(rt, 128 + a, H, q, 256), op=ALU.mult), [src_tb, rt], [t2])
            P.pool(lambda e: e.tensor_tensor(out=out_ap3, in0=t1v, in1=t2v, op=ALU.add), [t1, t2], [out_tb])

        def tr_store(items, dst_ap, dst_tb, nrows):
            for _ in tr_store_g(items, dst_ap, dst_tb, nrows):
                pass

        def tr_store_g(items, dst_ap, dst_tb, nrows):
            pt = P.ps()
            ptb = pt.h[:].bitcast(BF16)
            for k, (ap, tb) in enumerate(items):
                P.tr(pt, ptb[0:nrows, k * 128:(k + 1) * 128], ap, idb, [tb])
            yield
            stg = stg_ring()
            n = len(items)
            P.copy(stg[0:nrows, 0:n, :], ptb[0:nrows, 0:n * 128].rearrange("p (k t) -> p k t", k=n), [pt], [stg])
            P.dma("sp", dst_ap, stg[0:nrows, 0:n, :], [stg], [dst_tb])

        ph = ExitStack()
        P.stack = ph
        alloc_stage()
        w_in0 = P.sb([128, 8, L0C], BF16, "w_in0")
        for c in range(8):
            P.dma("pool", w_in0[:, c, :], I["w_in0"].h[c * 128:(c + 1) * 128, :], [], [w_in0])
        w_uq = P.sb([128, 3, 768], BF16, "w_uq")
        P.dma("pool", w_uq[:], I["w_uq"].h.rearrange("(c p) n -> p c n", p=128), [], [w_uq])
        w_ukv = P.sb([128, 2, 1024], BF16, "w_ukv")
        P.dma("pool", w_ukv[:], I["w_ukv"].h.rearrange("(c p) n -> p c n", p=128), [], [w_ukv])

        def kside0(*a):
            for _ in kside0_g(*a):
                pass

        def kside0_g(s, col, ckvn, ckvn_tb, kr_ap, kr_tb, kb_ap, kb_tb, vb_ap, vb_tb):
            pt = P.ps()
            ptb = pt.h[:].bitcast(BF16)
            for c in range(2):
                P.tr(pt, ptb[:, c * 128:(c + 1) * 128], ckvn[:, c * 128:(c + 1) * 128], idb, [ckvn_tb])
            yield
            ckT = stg_ring()
            P.copy(ckT[:, 0:2, :], ptb[:, 0:256].rearrange("p (k t) -> p k t", k=2), [pt], [ckT])
            yield
            psk, psv = P.ps(), P.ps()
            for c in range(2):
                P.mm(psk, psk[:, :], ckT[:, c, :], w_ukv[:, c, 0:512], c == 0, c == 1, [ckT, w_ukv])
            for c in range(2):
                P.mm(psv, psv[:, :], ckT[:, c, :], w_ukv[:, c, 512:1024], c == 0, c == 1, [ckT, w_ukv])
            yield
            ka = tb_ring()
            kav = ka[:, 0:768].rearrange("p (h d) -> p h d", h=8)
            P.act(lambda e: e.copy(out=kav[:, :, 0:64], in_=psk[:, :].rearrange("p (h d) -> p h d", h=8)), [psk], [ka])
            krt = tb_ring()
            P.dve(lambda e: e.tensor_copy(out=krt[:, 0:32], in_=kr_ap), [kr_tb], [krt])
            P.dve(lambda e: e.tensor_copy(out=kav[:, :, 64:96], in_=bc_mid(krt, 0, 8, 32, 768)), [krt], [ka])
            va = tb_ring()
            P.dve(lambda e: e.tensor_copy(out=va[:, 0:512], in_=psv[:, :]), [psv], [va])
            P.dma("sp", s.VA.h[col:col + 128, :], va[:, 0:512], [va], [s.VA])
            yield
            yield from tr_store_g([(kav[:, h, :], ka) for h in range(8)], s.KA.h[:, col:col + 128].rearrange("(h d) t -> d h t", h=8), s.KA, 96)
            yield
            yield from tr_store_g([(kb_ap, kb_tb)], s.KB.h[:, col:col + 128].rearrange("(k p) t -> p k t", k=1), s.KB, 128)
            P.dma("sp", s.VB.h[col:col + 128, :], vb_ap, [vb_tb], [s.VB])

        def l0_stage1(s):
            T = s.T

            def load_tile(i):
                r0 = i * 128
                xt = x_ring()
                P.dma("sp", xt[:], s.x0(r0), [], [xt])
                rtm = rts = None
                if not s.prompt:
                    rtm, rts = rt_ring(), rt_ring()
                    P.dma("sp", rtm[:, :, 0:32], I["rt_mla"].h[r0:r0 + 128, :, :], [], [rtm])
                    P.dma("sp", rts[:, :, 0:64], I["rt_swa"].h[r0:r0 + 128, :, :], [], [rts])
                return xt, rtm, rts
            def body(i, loaded):
                r0 = i * 128
                xt, rtm, rts = loaded
                hT = yield from build_hT_g(xt, 0, s.v)
                yield
                psa = proj(hT, w_in0, 0, 384)
                yield
                rs = rms_stat(psa[:, 0:384], psa, 384)
                yield
                cqn = tb_ring()
                P.dve(lambda e, psa=psa, rs=rs, cqn=cqn: e.scalar_tensor_tensor(out=cqn[:, 0:384], in0=psa[:, 0:384], scalar=rs[:], in1=bc0[:, 0:384],
                                                                                op0=ALU.mult, op1=ALU.mult), [psa, rs, bc0], [cqn])
                pt = P.ps()
                ptb = pt.h[:].bitcast(BF16)
                for c in range(3):
                    P.tr(pt, ptb[:, c * 128:(c + 1) * 128], cqn[:, c * 128:(c + 1) * 128], idb, [cqn])
                cqT = stg_ring()
                P.copy(cqT[:, 0:3, :], ptb[:, 0:384].rearrange("p (k t) -> p k t", k=3), [pt], [cqT])
                qa = tb_ring()
                qav = qa[:, 0:768].rearrange("p (h d) -> p h d", h=8)
                for g2 in range(2):
                    psq = P.ps()
                    for c in range(3):
                        P.mm(psq, psq[:, 0:384], cqT[:, c, :], w_uq[:, c, g2 * 384:(g2 + 1) * 384], c == 0, c == 2, [cqT, w_uq])
                    pv = psq[:, 0:384].rearrange("p (h d) -> p h d", h=4)
                    if s.prompt:
                        P.copy(qav[:, g2 * 4:(g2 + 1) * 4, :], pv, [psq], [qa])
                    else:
                        P.act(lambda e, pv=pv, g2=g2: e.copy(out=qav[:, g2 * 4:(g2 + 1) * 4, 0:64], in_=pv[:, :, 0:64]), [psq], [qa])
                        rope(pv[:, :, 64:96], psq, qav[:, g2 * 4:(g2 + 1) * 4, 64:96], qa, rtm, 4, 32)
                yield
                yield from tr_store_g([(qav[:, h, :], qa) for h in range(8)], s.QA.h[:, r0:r0 + 128].rearrange("(h d) t -> d h t", h=8), s.QA, 96)
                yield
                psb_ = proj(hT, w_in0, 384, 288)
                yield
                rs = rms_stat(psb_[:, 0:256], psb_, 256)
                yield
                ck32 = tm_ring()
                P.dve(lambda e, psb_=psb_, rs=rs, ck32=ck32: e.scalar_tensor_tensor(out=ck32[:, 0:256], in0=psb_[:, 0:256], scalar=rs[:], in1=bc0[:, 384:640],
                                                                                    op0=ALU.mult, op1=ALU.mult), [psb_, rs, bc0], [ck32])
                kr32 = tm_ring()
                if s.prompt:
                    P.act(lambda e, psb_=psb_, kr32=kr32: e.copy(out=kr32[:, 0:32], in_=psb_[:, 256:288]), [psb_], [kr32])
                    P.dma("sp", O["o_ckv"].h[s.i, r0:r0 + 128, :], ck32[:, 0:256], [ck32], [O["o_ckv"]])
                    P.dma("sp", O["o_kr"].h[s.i, r0:r0 + 128, :], kr32[:, 0:32], [kr32], [O["o_kr"]])
                else:
                    rope(psb_[:, 256:288].rearrange("p (h r) -> p h r", h=1), psb_, kr32[:, 0:32].rearrange("p (h r) -> p h r", h=1), kr32, rtm, 1, 32)
                ckn = tb_ring()
                P.pool(lambda e, ckn=ckn, ck32=ck32: e.tensor_copy(out=ckn[:, 0:256], in_=ck32[:, 0:256]), [ck32], [ckn])
                yield
                pse = proj(hT, w_in0, 1696, 256)
                yield
                kb = tb_ring()
                if s.prompt:
                    kv32 = tm_ring()
                    P.copy(kv32[:, 0:256], pse[:, 0:256], [pse], [kv32])
                    P.dma("sp", O["o_sk"].h[s.i, r0:r0 + 128, :], kv32[:, 0:128], [kv32], [O["o_sk"]])
                    P.dma("sp", O["o_sv"].h[s.i, r0:r0 + 128, :], kv32[:, 128:256], [kv32], [O["o_sv"]])
                    P.pool(lambda e, kb=kb, kv32=kv32: e.tensor_copy(out=kb[:, 0:256], in_=kv32[:, 0:256]), [kv32], [kb])
                else:
                    rope(pse[:, 0:128].rearrange("p (h r) -> p h r", h=2), pse, kb[:, 0:128].rearrange("p (h r) -> p h r", h=2), kb, rts, 2, 64)
                    P.act(lambda e, kb=kb, pse=pse: e.copy(out=kb[:, 128:256], in_=pse[:, 128:256]), [pse], [kb])
                yield
                yield from kside0_g(s, r0, ckn, ckn, kr32[:, 0:32], kr32, kb[:, 0:128], kb, kb[:, 128:256], kb)
                yield
                for zi, col0 in enumerate((672, 1952)):
                    psz = proj(hT, w_in0, col0, 512)
                    yield
                    zs = tb_ring()
                    P.act(lambda e, psz=psz, zs=zs: e.activation(out=zs[:, 0:512], in_=psz[:, :], func=AF.Silu), [psz], [zs])
                    yield
                    yield from tr_store_g([(zs[:, k * 128:(k + 1) * 128], zs) for k in range(4)],
                             s.ZT0.h[zi * 512:(zi + 1) * 512, r0:r0 + 128].rearrange("(k p) t -> p k t", p=128), s.ZT0, 128)
                    yield
                yield
                psd = proj(hT, w_in0, 1184, 512)
                yield
                qb = tb_ring()
                if s.prompt:
                    P.copy(qb[:, 0:512], psd[:, :], [psd], [qb])
                else:
                    rope(psd[:, :].rearrange("p (h r) -> p h r", h=8), psd, qb[:, 0:512].rearrange("p (h r) -> p h r", h=8), qb, rts, 8, 64)
                yield
                yield from tr_store_g([(qb[:, k * 128:(k + 1) * 128], qb) for k in range(4)],
                         s.QB.h[:, r0:r0 + 128].rearrange("(k p) t -> p k t", p=128), s.QB, 128)
            def post():
              if not s.prompt:
                for j in range(4):
                    c0 = j * 128
                    ckn, krb, kbb = tb_ring(), tb_ring(), tb_ring()
                    P.dma("pool", ckn[:, 0:256], I["c_ckv"].h[c0:c0 + 128, :], [], [ckn])
                    P.dma("pool", krb[:, 0:32], I["c_kr"].h[c0:c0 + 128, :], [], [krb])
                    P.dma("pool", kbb[:, 0:128], I["c_sk"].h[c0:c0 + 128, :], [], [kbb])
                    P.dma("pool", kbb[:, 128:256], I["c_sv"].h[c0:c0 + 128, :], [], [kbb])
                    kside0(s, s.T + c0, ckn, ckn, krb[:, 0:32], krb, kbb[:, 0:128], kbb, kbb[:, 128:256], kbb)
            return (T // 128, load_tile, body, post)

        RSD = {"i": 0, "t": [P.dram(f"rsd{j}", [1, 512], F32) for j in range(4)]}

        def alloc_att():
            R["kT"] = P.ring([128, 2560], BF16, "kT", 3)
            R["v"] = P.ring([128, 20, 128], BF16, "vt", 3)
            for _ in range(3):
                vb_ = R["v"]()
                P.dve(lambda e, vb_=vb_: e.memset(vb_[:, :, 64:128], 0.0), [], [vb_])
                P.dve(lambda e, vb_=vb_: e.memset(vb_[:, :, 64:65], 1.0), [], [vb_])
            R["q"] = P.ring([128, 512], BF16, "qT", 3)
            R["z"] = P.ring([128, 512], BF16, "zs", 3)
            R["p"] = P.ring([128, 512], BF16, "pT", 5)
            R["r"] = P.ring([128, 512], F32, "rs", 2)
            R["t"] = P.ring([128, 512], F32, "tt", 2)
            R["rbc"] = P.ring([128, 512], F32, "rbc", 2)
            R["og"] = P.ring([128, 512], BF16, "og", 2)
            R["swamask"] = P.sb([128, 6, 512], BF16, "swamask")
            P.dma("pool", R["swamask"][:], I["swamask"][:], [], [R["swamask"]])

        def alloc_out():
            alloc_stage(True)
            R["ogin"] = P.ring([128, 8, 128], BF16, "ogin", 2, 2)
            R["xo"] = P.ring([128, D], F32, "xo", 2, 2)
            R["w_out"] = P.sb([128, 8, D], BF16, "w_out")

        def attend(s, QT, KT, V, ZT, OG, nh, kvmap, Kd, dv, scale, zrow, ogrow, swa, sink):
            T, Tk = s.T, s.Tk
            QBK = min(512, T)
            nqb, nkt = T // QBK, Tk // 128
            swamask = R["swamask"]
            blocks = [(h, qb) for h in range(nh) for qb in range(nqb)]
            heads, pre = {}, {}

            ensured = set()

            def ensure(nb):
                if nb >= len(blocks) or nb in ensured:
                    return
                ensured.add(nb)
                h, qb = blocks[nb]
                if h not in heads:
                    kv = kvmap(h)
                    kT, vt = R["kT"](), R["v"]()
                    P.dma("sp", kT[0:Kd, 0:Tk], KT.h[kv * Kd:(kv + 1) * Kd, 0:Tk], [KT], [kT])
                    P.dma("sp", vt[:, 0:nkt, 0:dv], V.h[0:Tk, kv * dv:(kv + 1) * dv].rearrange("(n p) d -> p n d", p=128), [V], [vt])
                    heads[h] = (kT, vt)
                q0 = qb * QBK
                qT, zs = R["q"](), R["z"]()
                P.dma("sp", qT[0:Kd, 0:QBK], QT.h[h * Kd:(h + 1) * Kd, q0:q0 + QBK], [QT], [qT])
                P.dma("sp", zs[0:dv, 0:QBK], ZT.h[zrow(h):zrow(h) + dv, q0:q0 + QBK], [ZT], [zs])
                pre[nb] = (qT, zs)
            KK = Kd
            if Kd == 64:
                KK = 128
                for _ in range(3):
                    kb_, qb2_ = R["kT"](), R["q"]()
                    P.dve(lambda e, kb_=kb_: e.memset(kb_[64:128, :], 0.0), [], [kb_])
                    P.dve(lambda e, qb2_=qb2_: e.memset(qb2_[64:128, :], 0.0), [], [qb2_])

            def tiles_of(qb_):
                if swa and not s.prompt:
                    tl = [(kt, kt - 4 * qb_ + 1) for kt in range(4 * qb_ - 1, 4 * qb_ + 5) if 0 <= kt < T // 128]
                    return tl + [(kt, None) for kt in range(T // 128, nkt)]
                return [(kt, None) for kt in range(nkt)]

            def issue_qk_for(nb_, j, pend_):
                h_, qb_ = blocks[nb_]
                kT_, qT_ = heads[h_][0], pre[nb_][0]
                ps = P.ps()
                kt_ = tiles_of(qb_)[j][0]
                P.mm(ps, ps[:, 0:QBK], kT_[0:KK, kt_ * 128:(kt_ + 1) * 128], qT_[0:KK, 0:QBK], True, True, [kT_, qT_])
                pend_[j] = ps
            ensure(0)
            ensure(1)
            pend_next = {}
            for j in range(min(LOOK, len(tiles_of(blocks[0][1])))):
                issue_qk_for(0, j, pend_next)
            for nb, (h, qb) in enumerate(blocks):
                ensure(nb + 1)
                ensure(nb + 2)
                kT, vt = heads[h]
                if True:
                    q0 = qb * QBK
                    qT, zs = pre[nb]
                    tiles = tiles_of(qb)
                    aug = dv == 64
                    M = dv + 1 if aug else dv
                    pso = P.ps_acc()
                    pss = None if aug else P.ps_acc()
                    n = len(tiles)
                    pend = pend_next

                    def issue_qk(j, nb=nb, pend=pend):
                        issue_qk_for(nb, j, pend)
                    for idx, (kt, mk) in enumerate(tiles):
                        ps = pend.pop(idx)
                        pt = R["p"]()
                        P.act(lambda e, ps=ps, pt=pt: e.activation(out=pt[:, 0:QBK], in_=ps[:, 0:QBK], func=AF.Exp, scale=scale), [ps], [pt])
                        if mk is not None:
                            P.dve(lambda e, pt=pt, mk=mk: e.tensor_tensor(out=pt[:, 0:QBK], in0=pt[:, 0:QBK], in1=swamask[:, mk, 0:QBK], op=ALU.mult), [pt, swamask], [pt])
                        if idx + LOOK < n:
                            issue_qk(idx + LOOK)
                        a, b = idx == 0, idx == n - 1
                        MM_ = 128 if aug else M
                        P.mm(pso, pso[0:MM_, 0:QBK], vt[:, kt, 0:MM_], pt[:, 0:QBK], a, b, [vt, pt])
                        if not aug:
                            P.mm(pss, pss[0:dv, 0:QBK], ones_bf[:, 0:dv], pt[:, 0:QBK], a, b, [ones_bf, pt])
                    pend_next = {}
                    if nb + 1 < len(blocks):
                        for j in range(min(LOOK, len(tiles_of(blocks[nb + 1][1])))):
                            issue_qk_for(nb + 1, j, pend_next)
                    pre.pop(nb)
                    rs, tt, og = R["r"](), R["t"](), R["og"]()
                    if aug:
                        r1 = slice(dv, dv + 1)
                        if sink:
                            P.act(lambda e, rs=rs, pso=pso, h=h: e.activation(out=rs[r1, 0:QBK], in_=pso[r1, 0:QBK], func=AF.Ln, bias=esink[r1, h:h + 1]), [pso, esink], [rs])
                            P.act(lambda e, rs=rs: e.activation(out=rs[r1, 0:QBK], in_=rs[r1, 0:QBK], func=AF.Exp, scale=-1.0), [rs], [rs])
                        else:
                            P.dve(lambda e, rs=rs, pso=pso: e.reciprocal(out=rs[r1, 0:QBK], in_=pso[r1, 0:QBK]), [pso], [rs])
                        rbc = R["rbc"]()
                        RSD["i"] += 1
                        dsc = RSD["t"][RSD["i"] % 4]
                        P.dma("sp", dsc.h[0:1, 0:QBK], rs[r1, 0:QBK], [rs], [dsc])
                        P.dma("sp", rbc[0:dv, 0:QBK], bass.AP(dsc.h, 0, [[0, dv], [1, QBK]]), [dsc], [rbc])
                        P.dve(lambda e, tt=tt, pso=pso, zs=zs: e.tensor_tensor(out=tt[0:dv, 0:QBK], in0=pso[0:dv, 0:QBK], in1=zs[0:dv, 0:QBK], op=ALU.mult), [pso, zs], [tt])
                        P.pool(lambda e, og=og, tt=tt, rbc=rbc: e.tensor_tensor(out=og[0:dv, 0:QBK], in0=tt[0:dv, 0:QBK], in1=rbc[0:dv, 0:QBK], op=ALU.mult), [tt, rbc], [og])
                    else:
                        P.dve(lambda e, rs=rs, pss=pss: e.reciprocal(out=rs[0:dv, 0:QBK], in_=pss[0:dv, 0:QBK]), [pss], [rs])
                        P.dve(lambda e, tt=tt, pso=pso, zs=zs: e.tensor_tensor(out=tt[0:dv, 0:QBK], in0=pso[0:dv, 0:QBK], in1=zs[0:dv, 0:QBK], op=ALU.mult), [pso, zs], [tt])
                        P.pool(lambda e, og=og, tt=tt, rs=rs: e.tensor_tensor(out=og[0:dv, 0:QBK], in0=tt[0:dv, 0:QBK], in1=rs[0:dv, 0:QBK], op=ALU.mult), [tt, rs], [og])
                    P.dma("sp", OG.h[ogrow(h):ogrow(h) + dv, q0:q0 + QBK], og[0:dv, 0:QBK], [og], [OG])


        def outproj(s, l, OG, w_out, xsrc, xsrc_tb, Xdst, final):
            def load_tile(i):
                r0 = i * 128
                og, xt = R["ogin"](), x_ring()
                P.dma("sp", og[:], OG.h[:, r0:r0 + 128].rearrange("(c p) t -> p c t", p=128), [OG], [og])
                P.dma("sp", xt[:], xsrc(r0), [xsrc_tb] if xsrc_tb is not None else [], [xt])
                return og, xt
            def body(i, loaded):
                r0 = i * 128
                og, xt = loaded
                xo = R["xo"]()
                for hf in range(2):
                    ps = P.ps()
                    for c in range(8):
                        P.mm(ps, ps[:, :], og[:, c, :], w_out[:, c, hf * 512:(hf + 1) * 512], c == 0, c == 7, [og, w_out])
                    yield
                    tt = tm_ring()
                    P.dve(lambda e, tt=tt, ps=ps, hf=hf: e.tensor_tensor(out=tt[:, :], in0=ps[:, :], in1=gate_bc[l][s.v][:, hf * 512:(hf + 1) * 512], op=ALU.mult),
                          [ps, gate_bc[l][s.v]], [tt])
                    yield
                    P.pool(lambda e, tt=tt, xo=xo, xt=xt, hf=hf: e.tensor_tensor(out=xo[:, hf * 512:(hf + 1) * 512], in0=tt[:, :], in1=xt[:, hf * 512:(hf + 1) * 512], op=ALU.add),
                           [tt, xt], [xo])
                    yield
                if not final:
                    P.dma("sp", Xdst.h[r0:r0 + 128, :], xo[:], [xo], [Xdst])
                else:
                    junk, ss = junk_ring(), st_ring()
                    P.dve(lambda e, ss=ss: e.memset(ss[:], 0.0), [], [ss])
                    P.act(lambda e, junk=junk, xo=xo, ss=ss: e.activation(out=junk[:], in_=xo[:], func=AF.Square, accum_out=ss[:]), [xo], [junk, ss])
                    yield
                    P.act(lambda e, ss=ss: e.activation(out=ss[:], in_=ss[:], func=AF.Ln, scale=1.0 / D, bias=EPS), [ss], [ss])
                    P.act(lambda e, ss=ss: e.activation(out=ss[:], in_=ss[:], func=AF.Exp, scale=-0.5), [ss], [ss])
                    yield
                    yo = R["xo"]()
                    P.dve(lambda e, yo=yo, xo=xo, ss=ss: e.scalar_tensor_tensor(out=yo[:], in0=xo[:], scalar=ss[:], in1=bclnf[:], op0=ALU.mult, op1=ALU.mult),
                          [xo, ss, bclnf], [yo])
                    dst = O["y_p"].h[s.i, r0:r0 + 128, :] if s.prompt else O["y_s"].h[r0:r0 + 128, :]
                    P.dma("sp", dst, yo[:], [yo], [O["y_p"] if s.prompt else O["y_s"]])
            return (s.T // 128, load_tile, body, lambda: None)

        def phase_end():
            P.flush()
            ph.close()
            P.stack = st

        run_multi([l0_stage1(s) for s in seqs])
        phase_end()
        ph = ExitStack()
        P.stack = ph
        alloc_att()
        for s in seqs:
            attend(s, s.QA, s.KA, s.VA, s.ZT0, s.OG0, 8, lambda h: h, 96, 64, MLA_SCALE, lambda h: h * 64, lambda h: h * 64, False, False)
            attend(s, s.QB, s.KB, s.VB, s.ZT0, s.OG0, 8, lambda h: h // 4, 64, 64, SWA_SCALE, lambda h: 512 + h * 64, lambda h: 512 + h * 64, True, True)
        phase_end()
        ph = ExitStack()
        P.stack = ph
        alloc_out()
        P.dma("pool", R["w_out"][:], I["w_out0"].h.rearrange("(c p) n -> p c n", p=128), [], [R["w_out"]])
        run_multi([outproj(s, 0, s.OG0, R["w_out"], s.x0, None, s.X1, False) for s in seqs])
        phase_end()

        ph = ExitStack()
        P.stack = ph
        alloc_stage()
        w_in1 = P.sb([128, 8, L1C], BF16, "w_in1")
        for c in range(8):
            P.dma("pool", w_in1[:, c, :], I["w_in1"].h[c * 128:(c + 1) * 128, :], [], [w_in1])
        w_qkv = w_in1
        zero_t = P.sb([128, 1536], F32, "zero")
        P.dve(lambda e: e.memset(zero_t[:], 0.0), [], [zero_t])
        nega = P.sb([128, 8], F32, "nega")
        P.act(lambda e: e.activation(out=nega[:], in_=bc1[:, 0:8], func=AF.Exp), [bc1], [nega])
        P.dve(lambda e: e.tensor_scalar(out=nega[:], in0=nega[:], scalar1=-1.0, scalar2=None, op0=ALU.mult), [nega], [nega])

        def kside1(s, col, kd_ap, kd_tb, vd_ap, vd_tb):
            tr_store([(kd_ap[:, k * 128:(k + 1) * 128], kd_tb) for k in range(2)], s.KD.h[:, col:col + 128].rearrange("(k p) t -> p k t", p=128), s.KD, 128)
            P.dma("sp", s.VD.h[col:col + 128, :], vd_ap, [vd_tb], [s.VD])

        def l1_stage1(s):
            T = s.T
            gb = gb_all[s.i]
            P.dma("sp", s.QKV.h[0:1, :], zero_t[0:1, :], [zero_t], [s.QKV])
            P.dma("sp", s.QKV.h[T + 1:T + 2, :], zero_t[0:1, :], [zero_t], [s.QKV])
            def load_tile(i):
                r0 = i * 128
                xt = x_ring()
                P.dma("sp", xt[:], s.X1.h[r0:r0 + 128, :], [s.X1], [xt])
                rta = None
                if not s.prompt:
                    rta = rt_ring()
                    P.dma("sp", rta[:], I["rt_att"].h[r0:r0 + 128, :, :], [], [rta])
                return xt, rta
            def body(i, loaded):
                r0 = i * 128
                xt, rta = loaded
                hT = yield from build_hT_g(xt, 1, s.v)
                yield
                for g3 in range(3):
                    ps = proj(hT, w_qkv, g3 * 512, 512)
                    yield
                    t = tm_ring()
                    P.copy(t[:, :], ps[:, :], [ps], [t])
                    P.dma("sp", s.QKV.h[1 + r0:1 + r0 + 128, g3 * 512:(g3 + 1) * 512], t[:, :], [t], [s.QKV])
                yield
                ps = proj(hT, w_in1, 1536, 16)
                yield
                sp_ = st8_ring()
                P.dve(lambda e, sp_=sp_, ps=ps: e.tensor_tensor(out=sp_[:, 0:8], in0=ps[:, 0:8], in1=bc1[:, 8:16], op=ALU.add), [ps, bc1], [sp_])
                P.act(lambda e, sp_=sp_: e.activation(out=sp_[:, 0:8], in_=sp_[:, 0:8], func=AF.Exp), [sp_], [sp_])
                P.act(lambda e, sp_=sp_: e.activation(out=sp_[:, 0:8], in_=sp_[:, 0:8], func=AF.Ln, bias=1.0), [sp_], [sp_])
                P.dve(lambda e, sp_=sp_, gb=gb, i=i: e.tensor_tensor(out=gb[:, i, 0:8], in0=sp_[:, 0:8], in1=nega[:], op=ALU.mult), [sp_, nega], [gb])
                P.act(lambda e, ps=ps, gb=gb, i=i: e.activation(out=gb[:, i, 8:16], in_=ps[:, 8:16], func=AF.Exp, scale=-1.0), [ps], [gb])
                P.dve(lambda e, gb=gb, i=i: e.tensor_scalar(out=gb[:, i, 8:16], in0=gb[:, i, 8:16], scalar1=1.0, scalar2=None, op0=ALU.add), [gb], [gb])
                P.dve(lambda e, gb=gb, i=i: e.reciprocal(out=gb[:, i, 8:16], in_=gb[:, i, 8:16]), [gb], [gb])
                yield
                ps = proj(hT, w_in1, 2064, 512)
                yield
                rs = rms_stat(ps[:, :], ps, 128, 4)
                yield
                t = tm_ring()
                tv = t[:, :].rearrange("p (h d) -> p h d", h=4)
                P.dve(lambda e, tv=tv, ps=ps, rs=rs: e.tensor_tensor(out=tv, in0=ps[:, :].rearrange("p (h d) -> p h d", h=4), in1=bc_last(rs, 0, 4, 128, 8), op=ALU.mult), [ps, rs], [t])
                P.pool(lambda e, tv=tv: e.tensor_tensor(out=tv, in0=tv, in1=bc_mid(bc1, 144, 4, 128, 400), op=ALU.mult), [t, bc1], [t])
                qd = tb_ring()
                if s.prompt:
                    P.copy(qd[:, 0:512], t[:, :], [t], [qd])
                else:
                    rope(tv, t, qd[:, 0:512].rearrange("p (h d) -> p h d", h=4), qd, rta, 4, 128)
                yield
                yield from tr_store_g([(qd[:, k * 128:(k + 1) * 128], qd) for k in range(4)], s.QD.h[:, r0:r0 + 128].rearrange("(k p) t -> p k t", p=128), s.QD, 128)
                yield
                ps = proj(hT, w_in1, 2576, 512)
                yield
                rs = rms_stat(ps[:, 0:256], ps, 128, 2)
                yield
                t = tm_ring()
                tv = t[:, 0:256].rearrange("p (h d) -> p h d", h=2)
                P.dve(lambda e, tv=tv, ps=ps, rs=rs: e.tensor_tensor(out=tv, in0=ps[:, 0:256].rearrange("p (h d) -> p h d", h=2), in1=bc_last(rs, 0, 2, 128, 8), op=ALU.mult), [ps, rs], [t])
                P.pool(lambda e, tv=tv: e.tensor_tensor(out=tv, in0=tv, in1=bc_mid(bc1, 272, 2, 128, 400), op=ALU.mult), [t, bc1], [t])
                P.act(lambda e, t=t, ps=ps: e.copy(out=t[:, 256:512], in_=ps[:, 256:512]), [ps], [t])
                kd = tb_ring()
                if s.prompt:
                    P.dma("sp", O["o_ak"].h[s.i, r0:r0 + 128, :], t[:, 0:256], [t], [O["o_ak"]])
                    P.dma("sp", O["o_av"].h[s.i, r0:r0 + 128, :], t[:, 256:512], [t], [O["o_av"]])
                    P.copy(kd[:, 0:512], t[:, :], [t], [kd])
                else:
                    rope(tv, t, kd[:, 0:256].rearrange("p (h d) -> p h d", h=2), kd, rta, 2, 128)
                    P.dve(lambda e, kd=kd, t=t: e.tensor_copy(out=kd[:, 256:512], in_=t[:, 256:512]), [t], [kd])
                kside1(s, r0, kd[:, 0:256], kd, kd[:, 256:512], kd)
                yield
                ps = proj(hT, w_in1, 1552, 512)
                yield
                t = tm_ring()
                P.act(lambda e, t=t, ps=ps: e.activation(out=t[:, :], in_=ps[:, :], func=AF.Silu), [ps], [t])
                P.dma("sp", s.ZC.h[r0:r0 + 128, :], t[:, :], [t], [s.ZC])
                yield
                ps = proj(hT, w_in1, 3088, 512)
                yield
                zs = tb_ring()
                P.act(lambda e, ps=ps, zs=zs: e.activation(out=zs[:, 0:512], in_=ps[:, :], func=AF.Silu), [ps], [zs])
                yield
                yield from tr_store_g([(zs[:, k * 128:(k + 1) * 128], zs) for k in range(4)], s.ZT1.h[512:1024, r0:r0 + 128].rearrange("(k p) t -> p k t", p=128), s.ZT1, 128)
            def post():
              if not s.prompt:
                for j in range(4):
                    c0 = j * 128
                    kd = tb_ring()
                    P.dma("pool", kd[:, 0:256], I["c_ak"].h[c0:c0 + 128, :], [], [kd])
                    P.dma("pool", kd[:, 256:512], I["c_av"].h[c0:c0 + 128, :], [], [kd])
                    kside1(s, s.T + c0, kd[:, 0:256], kd, kd[:, 256:512], kd)
            return (T // 128, load_tile, body, post)

        run_multi([l1_stage1(s) for s in seqs])
        phase_end()
        ph = ExitStack()
        P.stack = ph
        alloc_stage(False, need_x=False)
        gm = P.sb([128, 8, 128], F32, "gmask")
        P.dma("sp", gm[:], I["gmask"][:], [], [gm])
        amask = P.sb([128, 4, 128], F32, "amask")
        for d_ in range(2):
            P.dve(lambda e, d_=d_: e.tensor_scalar(out=amask[:, d_, :], in0=gm[:, d_, :], scalar1=-1.0, scalar2=1.0e4, op0=ALU.add, op1=ALU.mult), [gm], [amask])
            P.dve(lambda e, d_=d_: e.tensor_scalar(out=amask[:, 2 + d_, :], in0=gm[:, 2 + (1 - d_), :], scalar1=-1.0, scalar2=-1.0e4, op0=ALU.add, op1=ALU.mult), [gm], [amask])
        cw = P.sb([128, 3, 1536], F32, "convw")
        P.dma("sp", cw[:], I["bcconv"].h.rearrange("p (k c) -> p k c", k=3), [], [cw])
        SstAll = [[P.sb([128, 128], F32, "Sst") for _ in range(8)] for _ in range(3)]
        xin_ring = P.ring([128, 3, 1536], F32, "xin", 1)
        y_ring = P.ring([128, 1536], F32, "ycv", 3)
        yt_ring = P.ring([128, 1536], F32, "ytmp", 1)
        chain_rings = [P.ring([128, 128], F32, "gfc", 8) for _ in range(4)]
        ded = [[P.sb([128, 128], BF16 if j in (1, 2, 3, 5) else F32, "gded") for j in range(8)] for _ in range(4)]
        SbfAll = [[P.sb([128, 128], BF16, "Sbf") for _ in range(8)] for _ in range(3)]
        vn_rings = [P.ring([128, 128], BF16, "vnb", 2) for _ in range(4)]
        kq_ring = P.ring([128, 8, 128], F32, "kqT", 3)
        o_ring = P.ring([128, 512], F32, "otile", 4)
        of_ring = P.ring([128, 512], F32, "ofl", 2)
        s4_ring = P.ring([128, 16], F32, "s4", 12)

        def gdn_job(s, d):
            Sst = SstAll[s.i]
            T = s.T
            NT = T // 128
            gb = gb_all[s.i]
            U, UT, UTs = gm[:, d, :], gm[:, 1 - d, :], gm[:, 2 + (1 - d), :]
            def load(k):
                r0 = order[k] * 128
                xin = xin_ring()
                for kk in range(3):
                    P.dma("sp", xin[:, kk, :], s.QKV.h[r0 + kk:r0 + kk + 128, :], [s.QKV], [xin])
                return xin

            def prepgen(k, xin, next_load):
                i = order[k]
                r0 = i * 128
                y, yt = y_ring(), yt_ring()
                P.dve(lambda e, y=y, xin=xin: e.tensor_tensor(out=y[:], in0=xin[:, 0, :], in1=cw[:, 0, :], op=ALU.mult), [xin, cw], [y])
                P.pool(lambda e, yt=yt, xin=xin: e.tensor_tensor(out=yt[:], in0=xin[:, 1, :], in1=cw[:, 1, :], op=ALU.mult), [xin, cw], [yt])
                yield 10
                P.dve(lambda e, y=y, yt=yt: e.tensor_tensor(out=y[:], in0=y[:], in1=yt[:], op=ALU.add), [y, yt], [y])
                yield 3
                P.pool(lambda e, yt=yt, xin=xin: e.tensor_tensor(out=yt[:], in0=xin[:, 2, :], in1=cw[:, 2, :], op=ALU.mult), [xin, cw], [yt])
                next_load()
                yield 10
                P.dve(lambda e, y=y, yt=yt: e.tensor_tensor(out=y[:], in0=y[:], in1=yt[:], op=ALU.add), [y, yt], [y])
                yield 3
                P.act(lambda e, y=y: e.activation(out=y[:], in_=y[:], func=AF.Silu), [y], [y])
                yield 3
                P.pool(lambda e, yt=yt, y=y: e.tensor_tensor(out=yt[:, 0:1024], in0=y[:, 0:1024], in1=y[:, 0:1024], op=ALU.mult), [y], [yt])
                yield 7
                ss = st8_ring()
                P.dve(lambda e, ss=ss, yt=yt: e.tensor_reduce(out=ss[:, 0:8], in_=yt[:, 0:1024].rearrange("p (h d) -> p h d", h=8), axis=AX.X, op=ALU.add), [yt], [ss])
                yield 2
                P.act(lambda e, ss=ss: e.activation(out=ss[:, 0:8], in_=ss[:, 0:8], func=AF.Ln, bias=EPS), [ss], [ss])
                yield 0
                P.act(lambda e, ss=ss: e.activation(out=ss[:, 0:8], in_=ss[:, 0:8], func=AF.Exp, scale=-0.5), [ss], [ss])
                yield 0
                yv = y[:, 0:1024].rearrange("p (h d) -> p h d", h=8)
                P.dve(lambda e, yv=yv, ss=ss: e.tensor_tensor(out=yv, in0=yv, in1=bc_last(ss, 0, 8, 128, 8), op=ALU.mult), [y, ss], [y])
                yield 2
                kq = kq_ring()
                for half in range(2):
                    pt = P.ps()
                    for h in range(4):
                        P.tr(pt, pt[:, h * 128:(h + 1) * 128], y[:, (half * 4 + h) * 128:(half * 4 + h + 1) * 128], idf, [y])
                    yield 1
                    P.copy(kq[:, :, :].rearrange("p (h two) t -> p h two t", two=2)[:, :, 1 - half, :], pt[:, :].rearrange("p (h t) -> p h t", h=4), [pt], [kq])
                    yield 0
                g4 = gb[:, i, d * 4:(d + 1) * 4]
                be4 = gb[:, i, 8 + d * 4:8 + (d + 1) * 4]
                psc = P.ps()
                P.mm(psc, psc[:, 0:4], U, g4, True, True, [gm, gb])
                P.mm(psc, psc[:, 4:8], gm[:, 4, :], g4, True, True, [gm, gb])
                P.mm(psc, psc[:, 8:12], gm[:, 5, :], g4, True, True, [gm, gb])
                P.mm(psc, psc[:, 12:16], gm[:, 6, :], g4, True, True, [gm, gb])
                yield 1
                sc = s4_ring()
                ex = s4_ring()
                P.dve(lambda e, sc=sc, psc=psc: e.tensor_copy(out=sc[:, 0:4], in_=psc[:, 0:4]), [psc], [sc])
                P.act(lambda e, sc=sc, psc=psc: e.activation(out=sc[:, 8:16], in_=psc[:, 8:16], func=AF.Exp), [psc], [sc])
                yield 0
                P.dve(lambda e, sc=sc, psc=psc: e.tensor_tensor(out=sc[:, 4:8], in0=psc[:, 4:8], in1=sc[:, 0:4], op=ALU.subtract), [psc, sc], [sc])
                P.act(lambda e, sc=sc, ex=ex: e.activation(out=ex[:, 0:4], in_=sc[:, 0:4], func=AF.Exp), [sc], [ex])
                yield 0
                P.act(lambda e, sc=sc: e.activation(out=sc[:, 4:8], in_=sc[:, 4:8], func=AF.Exp), [sc], [sc])
                P.dve(lambda e, ex=ex, be4=be4: e.tensor_tensor(out=ex[:, 4:8], in0=ex[:, 0:4], in1=be4, op=ALU.mult), [ex, gb], [ex])
                Xs[k] = dict(i=i, r0=r0, y=y, kq=kq, g4=g4, be4=be4, sc=sc, ex=ex)

            def chain(h, X, ot, sl):
                i, y, kq, g4, be4, sc, ex = X['i'], X['y'], X['kq'], X['g4'], X['be4'], X['sc'], X['ex']
                fr = chain_rings[sl]
                bank = [0]

                def cps():
                    bank[0] += 1
                    return P.psum_banks[4 + sl]
                cb = d * 4 + h
                kT, qT = kq[:, 2 * h, :], kq[:, 2 * h + 1, :]
                k_tok, v_tok = y[:, (4 + h) * 128:(5 + h) * 128], y[:, (8 + h) * 128:(9 + h) * 128]
                psA = cps()
                P.mm(psA, psA[:, 0:256], kT, kq[:, 2 * h:2 * h + 2, :], True, True, [kq])
                grep = fr()
                P.dve(lambda e, grep=grep, h=h: e.tensor_scalar(out=grep[:], in0=ones_f[:], scalar1=g4[:, h:h + 1], scalar2=None, op0=ALU.mult), [ones_f, gb], [grep])
                yield
                P.mm(psA, psA[:, 256:384], grep[:], U, True, True, [grep, gm])
                yield
                EB = ded[sl][0]
                P.act(lambda e, EB=EB, psA=psA: e.activation(out=EB[:], in_=psA[:, 256:384], func=AF.Exp), [psA], [EB])
                DT, Dcs = fr(), fr()
                P.dve(lambda e, DT=DT, psA=psA, sc=sc, h=h: e.scalar_tensor_tensor(out=DT[:], in0=psA[:, 256:384], scalar=sc[:, h:h + 1], in1=amask[:, d, :],
                                                                                 op0=ALU.subtract, op1=ALU.add), [psA, sc, amask], [DT])
                P.dve(lambda e, Dcs=Dcs, psA=psA, sc=sc, h=h: e.scalar_tensor_tensor(out=Dcs[:], in0=psA[:, 256:384], scalar=sc[:, h:h + 1], in1=amask[:, 2 + d, :],
                                                                                   op0=ALU.subtract, op1=ALU.add), [psA, sc, amask], [Dcs])
                yield
                P.act(lambda e, DT=DT: e.activation(out=DT[:], in_=DT[:], func=AF.Exp), [DT], [DT])
                P.act(lambda e, Dcs=Dcs: e.activation(out=Dcs[:], in_=Dcs[:], func=AF.Exp, scale=-1.0), [Dcs], [Dcs])
                kdc, qg, vb_, kbg = ded[sl][2], ded[sl][3], ded[sl][6], ded[sl][7]
                P.dve(lambda e, vb_=vb_, h=h: e.tensor_scalar(out=vb_[:], in0=v_tok, scalar1=be4[:, h:h + 1], scalar2=None, op0=ALU.mult), [y, gb], [vb_])
                P.dve(lambda e, kbg=kbg, ex=ex, h=h: e.tensor_scalar(out=kbg[:], in0=k_tok, scalar1=ex[:, 4 + h:5 + h], scalar2=None, op0=ALU.mult), [y, ex], [kbg])
                yield
                Pk, AT = fr(), ded[sl][1]
                P.dve(lambda e, Pk=Pk, psA=psA, Dcs=Dcs, h=h: e.scalar_tensor_tensor(out=Pk[:], in0=psA[:, 0:128], scalar=be4[:, h:h + 1], in1=Dcs[:], op0=ALU.mult, op1=ALU.mult),
                      [psA, gb, Dcs], [Pk])
                P.dve(lambda e, AT=AT, psA=psA, DT=DT: e.scalar_tensor_tensor(out=AT[:], in0=psA[:, 128:256], scalar=GDN_SCALE, in1=DT[:], op0=ALU.mult, op1=ALU.mult),
                      [psA, DT], [AT])
                yield
                psB = cps()
                P.tr(psB, psB[:, 0:128], Pk[:], idf, [Pk])
                P.dve(lambda e, kdc=kdc, sc=sc, h=h: e.tensor_scalar(out=kdc[:], in0=k_tok, scalar1=sc[:, 4 + h:5 + h], scalar2=None, op0=ALU.mult), [y, sc], [kdc])
                P.dve(lambda e, qg=qg, EB=EB: e.scalar_tensor_tensor(out=qg[:], in0=qT, scalar=GDN_SCALE, in1=EB[:], op0=ALU.mult, op1=ALU.mult), [kq, EB], [qg])
                yield
                Qk, Tt = fr(), fr()
                P.act(lambda e, Qk=Qk, psB=psB: e.copy(out=Qk[:], in_=psB[:, 0:128]), [psB], [Qk])
                yield
                P.dve(lambda e, Tt=Tt, Qk=Qk: e.tensor_tensor(out=Tt[:], in0=gm[:, 7, :], in1=Qk[:], op=ALU.subtract), [gm, Qk], [Tt])
                for lev in range(1, 6):
                    psC = cps()
                    P.mm(psC, psC[:, 0:128], Qk[:], Pk[:], True, True, [Qk, Pk])
                    yield
                    Pn = fr()
                    P.act(lambda e, Pn=Pn, psC=psC: e.copy(out=Pn[:], in_=psC[:, 0:128]), [psC], [Pn])
                    yield
                    if lev < 5:
                        P.tr(psC, psC[:, 128:256], Pn[:], idf, [Pn])
                        Qn = fr()
                    psD = cps()
                    P.mm(psD, psD[:, 0:128], Pn[:], Tt[:], True, True, [Pn, Tt])
                    yield
                    if lev < 5:
                        P.act(lambda e, Qn=Qn, psC=psC: e.copy(out=Qn[:], in_=psC[:, 128:256]), [psC], [Qn])
                    Tn = fr()
                    P.dve(lambda e, Tn=Tn, psD=psD, Tt=Tt: e.tensor_tensor(out=Tn[:], in0=psD[:, 0:128], in1=Tt[:], op=ALU.add), [psD, Tt], [Tn])
                    yield
                    Pk, Tt = Pn, Tn
                    if lev < 5:
                        Qk = Qn
                psE = cps()
                P.mm(psE, psE[:, 0:128], Tt[:], vb_[:], True, True, [Tt, vb_])
                P.mm(psE, psE[:, 128:256], kbg[:], Tt[:], True, True, [kbg, Tt])
                yield
                u, wT = ded[sl][4], ded[sl][5]
                P.act(lambda e, u=u, psE=psE: e.copy(out=u[:], in_=psE[:, 0:128]), [psE], [u])
                P.dve(lambda e, wT=wT, psE=psE: e.tensor_copy(out=wT[:], in_=psE[:, 128:256]), [psE], [wT])
                yield
                Stb = Sst[cb]
                Sc = Stb[:, :]
                Sb = SbfAll[s.i][cb]
                for c in ((0, 1) if d == 0 else (1, 0)):
                    rc = slice(c * 64, (c + 1) * 64)
                    ps1 = cps()
                    P.mm(ps1, ps1[:, 0:128], wT[:], Sb[:], True, True, [wT, Sb])
                    P.mm(ps1, ps1[:, 128:256], qg[:], Sb[:], True, False, [qg, Sb])
                    yield
                    vn = vn_rings[sl]()
                    P.dve(lambda e, vn=vn, u=u, ps1=ps1: e.tensor_tensor(out=vn[:], in0=u[:], in1=ps1[:, 0:128], op=ALU.subtract), [u, ps1], [vn])
                    yield
                    P.mm(ps1, ps1[:, 128:256], AT[:], vn[:], False, True, [AT, vn])
                    P.mm(ps1, ps1[:, 256:384], kdc[rc, :], vn[rc, :], True, True, [kdc, vn])
                    yield
                    P.act(lambda e, ot=ot, ps1=ps1, rc=rc, h=h: e.copy(out=ot[rc, h * 128:(h + 1) * 128], in_=ps1[rc, 128:256]), [ps1], [ot])
                    P.dve(lambda e, ps1=ps1, sc=sc, c=c, h=h, Sc=Sc: e.scalar_tensor_tensor(out=Sc, in0=Sc, scalar=sc[:, 8 + c * 4 + h:9 + c * 4 + h], in1=ps1[:, 256:384],
                                                                                            op0=ALU.mult, op1=ALU.add), [Stb, sc, ps1], [Stb])
                    yield
                    P.pool(lambda e, Sb=Sb, Stb=Stb: e.tensor_copy(out=Sb[:], in_=Stb[:]), [Stb], [Sb])
                    yield

            order = list(range(NT)) if d == 0 else list(range(NT - 1, -1, -1))
            Xs, ots, remaining = {}, {}, {}

            def finish_tile(k):
                i = order[k]
                r0 = i * 128
                ot = ots[k]
                if d == 0:
                    P.dma("sp", s.OF.h[r0:r0 + 128, :], ot[:], [ot], [s.OF])
                else:
                    of_, zc = of_ring(), tm_ring()
                    P.dma("sp", of_[:], s.OF.h[r0:r0 + 128, :], [s.OF], [of_])
                    P.dma("sp", zc[:], s.ZC.h[r0:r0 + 128, :], [s.ZC], [zc])
                    P.dve(lambda e, ot=ot, of_=of_: e.tensor_tensor(out=ot[:], in0=ot[:], in1=of_[:], op=ALU.add), [ot, of_], [ot])
                    rs = rms_stat(ot[:], ot, 128, 4)
                    ov = ot[:].rearrange("p (h d) -> p h d", h=4)
                    P.dve(lambda e, ov=ov, rs=rs: e.tensor_tensor(out=ov, in0=ov, in1=bc_last(rs, 0, 4, 128, 8), op=ALU.mult), [ot, rs], [ot])
                    P.pool(lambda e, ov=ov: e.tensor_tensor(out=ov, in0=ov, in1=bc_mid(bc1, 16, 4, 128, 400), op=ALU.mult), [ot, bc1], [ot])
                    ogb = tb_ring()
                    P.dve(lambda e, ogb=ogb, ot=ot, zc=zc: e.tensor_tensor(out=ogb[:, 0:512], in0=ot[:], in1=zc[:], op=ALU.mult), [ot, zc], [ogb])
                    tr_store([(ogb[:, k * 128:(k + 1) * 128], ogb) for k in range(4)], s.OG1.h[0:512, r0:r0 + 128].rearrange("(k p) t -> p k t", p=128), s.OG1, 128)


            return dict(NT=NT, load=load, prepgen=prepgen, chain=chain, fin=finish_tile, Xs=Xs, ots=ots, rem=remaining)

        for s in seqs:
            for cb in range(8):
                if s.prompt:
                    P.dve(lambda e, cb=cb, s=s: e.memset(SstAll[s.i][cb][:], 0.0), [], [SstAll[s.i][cb]])
                else:
                    P.dma("sp", SstAll[s.i][cb][:], I["c_gdn"].h[cb // 4, cb % 4], [], [SstAll[s.i][cb]])
                P.pool(lambda e, cb=cb, s=s: e.tensor_copy(out=SbfAll[s.i][cb][:], in_=SstAll[s.i][cb][:]), [SstAll[s.i][cb]], [SbfAll[s.i][cb]])
        jobs = [gdn_job(s, 0) for s in seqs] + [gdn_job(s, 1) for s in seqs]
        tiles = [(ji, k) for ji, J in enumerate(jobs) for k in range(J["NT"])]
        queue = [(ti, h) for ti in range(len(tiles)) for h in range(4)]
        loaded, done_prep = {}, set()

        def start_load(ti):
            if ti < len(tiles) and ti not in loaded:
                loaded[ti] = jobs[tiles[ti][0]]["load"](tiles[ti][1])
        start_load(0)
        active, free_slots = [], [0, 1, 2, 3]
        qi, rnd, GAP = 0, 0, 1
        prep_i, prep_gen, prep_wait, cur_prep, finished_tiles = 0, None, 0, None, 0
        while qi < len(queue) or active or prep_gen is not None:
            if prep_gen is None and prep_i < len(tiles) and prep_i - finished_tiles <= 2:
                cur_prep = prep_i
                ji, k = tiles[prep_i]
                start_load(prep_i)
                prep_gen = jobs[ji]["prepgen"](k, loaded.pop(prep_i), lambda nxt=prep_i + 1: start_load(nxt))
                prep_i += 1
                prep_wait = 0
            if prep_gen is not None:
                if prep_wait > 0:
                    prep_wait -= 1
                else:
                    try:
                        prep_wait = next(prep_gen) or 0
                    except StopIteration:
                        done_prep.add(cur_prep)
                        prep_gen = None
            if qi < len(queue) and free_slots and rnd % GAP == 0 and queue[qi][0] in done_prep:
                ti, h = queue[qi]
                qi += 1
                ji, k = tiles[ti]
                J = jobs[ji]
                if k not in J["ots"]:
                    J["ots"][k] = o_ring()
                    J["rem"][k] = 4
                sl = free_slots.pop(0)
                active.append((ji, k, h, sl, J["chain"](h, J["Xs"][k], J["ots"][k], sl)))
            for item in list(active):
                ji, k, h, sl, g_ = item
                try:
                    next(g_)
                except StopIteration:
                    active.remove(item)
                    free_slots.append(sl)
                    J = jobs[ji]
                    J["rem"][k] -= 1
                    if J["rem"][k] == 0:
                        J["fin"](k)
                        finished_tiles += 1
            rnd += 1
        for s in seqs:
            if s.prompt:
                for cb in range(8):
                    P.dma("sp", O["o_gdn"].h[s.i, cb // 4, cb % 4], SstAll[s.i][cb][:], [SstAll[s.i][cb]], [O["o_gdn"]])
        phase_end()
        ph = ExitStack()
        P.stack = ph
        alloc_att()
        for s in seqs:
            attend(s, s.QD, s.KD, s.VD, s.ZT1, s.OG1, 4, lambda h: h // 2, 128, 128, ATT_SCALE, lambda h: 512 + h * 128, lambda h: 512 + h * 128, False, False)
        phase_end()
        ph = ExitStack()
        P.stack = ph
        alloc_out()
        P.dma("pool", R["w_out"][:], I["w_out1"].h.rearrange("(c p) n -> p c n", p=128), [], [R["w_out"]])
        run_multi([outproj(s, 1, s.OG1, R["w_out"], lambda r0, s=s: s.X1.h[r0:r0 + 128, :], s.X1, None, True) for s in seqs])
        phase_end()
        P.finish()
    return nc


def _rope_host(n_tok, rot):
    q = rot // 4
    inv = (10000.0 ** (-np.arange(q, dtype=np.float32) / q)).astype(np.float32)
    t = np.arange(n_tok)
    pos = np.stack([t // 64, t % 64], -1).astype(np.float32)
    ang = pos[:, :, None] * inv
    c, s_ = np.cos(ang).astype(np.float32), np.sin(ang).astype(np.float32)
    Cx = np.stack([c, c], 2).reshape(n_tok, rot)
    Sx = np.stack([-s_, s_], 2).reshape(n_tok, rot)
    return np.ascontiguousarray(np.stack([Cx, Sx], 1)).astype(np.float32)


_NC = None


def kernel(**inp):
    global _NC
    f = lambda a: np.ascontiguousarray(np.asarray(a, dtype=np.float32))
    x_prompt, x_sample = f(inp["x_prompt"]), f(inp["x_sample"])
    fm = lambda v: np.ascontiguousarray(v.reshape(-1, 128).T)
    rep = lambda v: np.ascontiguousarray(np.broadcast_to(v.reshape(1, -1), (128, v.size)))
    b0, b1 = f(inp["b_mod0"]), f(inp["b_mod1"])
    shared = {
        "w_mod0": f(inp["w_mod0"]), "w_mod1": f(inp["w_mod1"]),
        "bm_fm": np.ascontiguousarray(np.stack([fm(b0[:2048]), fm(b1[:2048])], 1)),
        "bm_row": np.ascontiguousarray(np.broadcast_to(np.stack([b0[2048:], b1[2048:]], 0)[None], (2, 2, D))),
        "ln_fm": np.ascontiguousarray(np.stack([fm(f(inp["ln0"])), fm(f(inp["ln1"]))], 1)),
        "w_in0": f(inp["w_in0"]), "w_in1": f(inp["w_in1"]), "w_uq": f(inp["w_uq"]),
        "w_out0": f(inp["w_out0"]), "w_out1": f(inp["w_out1"]),
        "bc0": rep(np.concatenate([f(inp["mla_q_norm"]), f(inp["mla_kv_norm"]), f(inp["swa_sink"])])),
        "bc1": rep(np.concatenate([f(inp["gdn_a_log"]).reshape(-1), f(inp["gdn_dt_bias"]).reshape(-1), f(inp["gdn_norm"]),
                                   f(inp["att_q_norm"]), f(inp["att_k_norm"])])),
        "bcconv": rep(f(inp["gdn_conv"]).reshape(-1)), "bclnf": rep(f(inp["ln_f"])),
        "rt_mla": _rope_host(2048, 32), "rt_swa": _rope_host(2048, 64), "rt_att": _rope_host(2048, 128),
        "ident": np.eye(128, dtype=np.float32),
    }
    wk = f(inp["w_ukv"]).reshape(256, 8, 128)
    shared["w_ukv"] = np.ascontiguousarray(np.concatenate([wk[:, :, :64].reshape(256, 512), wk[:, :, 64:].reshape(256, 512)], 1))
    sel = np.zeros((2, 2, 128), np.float32)
    sel[0, 0], sel[1, 1] = 1, 1
    shared["sel"] = sel
    r, c = np.arange(128)[:, None], np.arange(512)[None, :]
    shared["swamask"] = np.ascontiguousarray(np.stack([(np.abs(c - ((mk - 1) * 128 + r)) <= 128) for mk in range(6)], 1).astype(np.float32))
    m, i_ = np.arange(128)[:, None], np.arange(128)[None, :]
    same = (m // 64) == (i_ // 64)
    Uf, Ub = same & (m <= i_), same & (m >= i_)
    eye = np.eye(128, dtype=bool)
    gms = [Uf, Ub, Uf & ~eye, Ub & ~eye, same, np.broadcast_to(m < 64, (128, 128)), np.broadcast_to(m >= 64, (128, 128)), eye]
    shared["gmask"] = np.ascontiguousarray(np.stack([g.astype(np.float32) for g in gms], 1))
    c_all, c_ctx = f(inp["c"]), f(inp["c_ctx"])
    in_maps = []
    for b in range(8):
        mp = dict(shared)
        mp["xp"] = np.ascontiguousarray(x_prompt[2 * b:2 * b + 2])
        mp["xs"] = np.ascontiguousarray(x_sample[b])
        mp["c_ckv"] = f(inp["cache_l0_mla_ckv"][b])
        mp["c_kr"] = f(inp["cache_l0_mla_krope"][b])
        mp["c_sk"] = f(inp["cache_l0_swa_k"][b]).reshape(512, 128)
        mp["c_sv"] = f(inp["cache_l0_swa_v"][b]).reshape(512, 128)
        mp["c_gdn"] = f(inp["state_l1_gdn"][b])
        mp["c_ak"] = f(inp["cache_l1_attn_k"][b]).reshape(512, 256)
        mp["c_av"] = f(inp["cache_l1_attn_v"][b]).reshape(512, 256)
        mp["cT"] = np.ascontiguousarray(np.stack([fm(c_ctx), fm(c_all[b])], -1))
        in_maps.append(mp)
    if _NC is None:
        _NC = build_program()
    res = run_bass_kernel_spmd(_NC, in_maps, core_ids=list(range(8)))
    R = res.results
    if DEBUG:
        DBG_OUT["x1"] = np.asarray(R[0]["dbg_x1"])
        DBG_OUT["og0"] = np.asarray(R[0]["dbg_og0"])
        for k in ("qa", "ka", "va", "zt0"):
            DBG_OUT[k] = np.asarray(R[0]["dbg_" + k]).astype(np.float32)
    cat = lambda k: np.concatenate([np.asarray(r_[k], dtype=np.float32) for r_ in R], 0)
    y_p = cat("y_p")
    y_s = np.stack([np.asarray(r_["y_s"], dtype=np.float32) for r_ in R], 0)
    return (y_p, y_s, cat("o_ckv"), cat("o_kr"), cat("o_sk").reshape(16, 256, 2, 64), cat("o_sv").reshape(16, 256, 2, 64),
            cat("o_gdn"), cat("o_ak").reshape(16, 256, 2, 128), cat("o_av").reshape(16, 256, 2, 128))
```
